# Optimizing a Trainium2 kernel written in Bass

```python
import math
import jax, jax.numpy as jnp
from jax import lax
import numpy as np

D_MODEL = 1024
BATCH = 8
SEQ = 2048
DEPTH = 1
DEC_BATCH = 128
DEC_SEQ = 8
PAST_LEN = 16384
PAGE_SIZE = 128

MIX_W = D_MODEL
CONV_W = MIX_W // 2
RWKV_W = MIX_W - CONV_W
HEAD_DIM = 64
N_HEADS_RWKV = RWKV_W // HEAD_DIM
N_CONV_GROUPS = CONV_W // HEAD_DIM
CONV_K = 3
DECAY_LORA = 64
A_LORA = 64
CONV_COLS = 4 * CONV_W
RWKV_COLS = 4 * RWKV_W + DECAY_LORA + A_LORA
IN_COLS = CONV_COLS + RWKV_COLS
RMS_EPS = 1e-6
GN_EPS = 64e-5
DECAY_SCALE = math.exp(-0.5)

kernel_name = "hybrid_shortconv_rwkv7_decode_step"


def rms_norm(x, g):
    xf = x.astype(jnp.float32)
    y = xf * lax.rsqrt(jnp.mean(xf * xf, axis=-1, keepdims=True) + RMS_EPS)
    return (y * g.astype(jnp.float32)).astype(x.dtype)


def wkv7_scan(r, w, k, v, kk, a, s0):
    def step(S, inp):
        r_t, w_t, k_t, v_t, kk_t, a_t = inp
        sa = jnp.einsum('bhij,bhj->bhi', S, -kk_t)
        S = (S * w_t[:, :, None, :] + sa[..., :, None] * (kk_t * a_t)[..., None, :]
             + v_t[..., :, None] * k_t[..., None, :])
        o = jnp.einsum('bhij,bhj->bhi', S, r_t)
        return S, o
    xs = tuple(jnp.swapaxes(t, 0, 1) for t in (r, w, k, v, kk, a))
    s_fin, o = lax.scan(step, s0, xs)
    return jnp.swapaxes(o, 0, 1), s_fin


def mixer_layer(x, conv_buf, shift_prev, wkv0, norm_g, w_in, mu_shift, conv_w, w_dec2, w0,
                w_a2, a0, k_k, k_a, r_k, lnx_g, lnx_b, w_out):
    bsz, T, _ = x.shape
    xn = rms_norm(x, norm_g)
    p = xn @ w_in
    pc, pr = p[..., :CONV_COLS], p[..., CONV_COLS:]

    h = pc[..., 0:CONV_W]
    bg = pc[..., CONV_W:2 * CONV_W]
    cg = pc[..., 2 * CONV_W:3 * CONV_W]
    gc = pc[..., 3 * CONV_W:]
    u = cg * h
    u_ext = jnp.concatenate([conv_buf.astype(u.dtype), u], axis=1)
    z = (conv_w[0] * u_ext[:, 0:T] + conv_w[1] * u_ext[:, 1:T + 1]
         + conv_w[2] * u_ext[:, 2:T + 2])
    y_conv = bg * z * jax.nn.silu(gc)
    new_conv = u_ext[:, -(CONV_K - 1):]

    pr_first = shift_prev.astype(xn.dtype) @ w_in[:, CONV_COLS:]
    pr_prev = jnp.concatenate([pr_first[:, None], pr[:, :-1]], axis=1)
    pr = pr + mu_shift * (pr_prev - pr)
    new_shift = xn[:, -1]
    o0 = 3 * RWKV_W
    r = pr[..., 0:RWKV_W]
    k = pr[..., RWKV_W:2 * RWKV_W]
    v = pr[..., 2 * RWKV_W:o0]
    wl = pr[..., o0:o0 + DECAY_LORA]
    al = pr[..., o0 + DECAY_LORA:o0 + DECAY_LORA + A_LORA]
    gr = pr[..., o0 + DECAY_LORA + A_LORA:]

    f32 = jnp.float32
    decay = jnp.exp(-DECAY_SCALE * jax.nn.sigmoid((w0 + jnp.tanh(wl) @ w_dec2).astype(f32)))
    a = jax.nn.sigmoid((a0 + al @ w_a2).astype(f32))
    hs = (bsz, T, N_HEADS_RWKV, HEAD_DIM)
    rf = r.astype(f32).reshape(hs)
    kf = k.astype(f32)
    vf = v.astype(f32).reshape(hs)
    kk = (kf * k_k.astype(f32)).reshape(hs)
    kk = kk / jnp.maximum(jnp.sqrt(jnp.sum(kk * kk, axis=-1, keepdims=True)), 1e-12)
    kf = (kf * (1.0 + (a - 1.0) * k_a.astype(f32))).reshape(hs)
    af = a.reshape(hs)
    wf = decay.reshape(hs)
    o, s_fin = wkv7_scan(rf, wf, kf, vf, kk, af, wkv0.astype(f32))
    mean = jnp.mean(o, axis=-1, keepdims=True)
    var = jnp.mean(jnp.square(o - mean), axis=-1, keepdims=True)
    on = ((o - mean) * lax.rsqrt(var + GN_EPS)).reshape(bsz, T, RWKV_W)
    on = on * lnx_g.astype(f32) + lnx_b.astype(f32)
    bonus = jnp.sum(rf * kf * r_k.astype(f32), axis=-1, keepdims=True) * vf
    y_rwkv = (on + bonus.reshape(bsz, T, RWKV_W)).astype(x.dtype) * jax.nn.silu(gr)

    y = jnp.concatenate([y_conv, y_rwkv], axis=-1) @ w_out
    return x + y, new_conv, new_shift, s_fin.astype(wkv0.dtype)


def setup_inputs(seed: int = 0) -> dict:
    key = jax.random.key(seed)
    ks = jax.random.split(key, 24)
    f = jnp.float32
    nrm = lambda i, shape, s: jax.random.normal(ks[i], shape, f) * s
    return {
        "x_prompt": nrm(0, (BATCH, SEQ, D_MODEL), 1.0),
        "x_sample": nrm(1, (DEC_BATCH, DEC_SEQ, D_MODEL), 1.0),
        "state_conv": nrm(2, (DEPTH, DEC_BATCH, CONV_K - 1, CONV_W), 1.0),
        "state_shift": nrm(3, (DEPTH, DEC_BATCH, D_MODEL), 1.0),
        "state_wkv": nrm(4, (DEPTH, DEC_BATCH, N_HEADS_RWKV, HEAD_DIM, HEAD_DIM), 0.5),
        "norm_g": 1.0 + nrm(5, (DEPTH, D_MODEL), 0.02),
        "w_in": nrm(6, (DEPTH, D_MODEL, IN_COLS), D_MODEL ** -0.5),
        "mu_shift": jax.random.uniform(ks[7], (DEPTH, RWKV_COLS), f, 0.0, 1.0),
        "conv_w": nrm(8, (DEPTH, CONV_K, CONV_W), CONV_K ** -0.5),
        "w_dec2": nrm(9, (DEPTH, DECAY_LORA, RWKV_W), 0.1),
        "w0": nrm(10, (DEPTH, RWKV_W), 0.5) + 1.0,
        "w_a2": nrm(11, (DEPTH, A_LORA, RWKV_W), 0.1),
        "a0": nrm(12, (DEPTH, RWKV_W), 0.1),
        "k_k": 0.85 + nrm(13, (DEPTH, RWKV_W), 0.05),
        "k_a": 1.0 + nrm(14, (DEPTH, RWKV_W), 0.05),
        "r_k": nrm(15, (DEPTH, N_HEADS_RWKV, HEAD_DIM), 0.1),
        "lnx_g": 1.0 + nrm(16, (DEPTH, RWKV_W), 0.02),
        "lnx_b": nrm(17, (DEPTH, RWKV_W), 0.02),
        "w_out": nrm(18, (DEPTH, MIX_W, D_MODEL), MIX_W ** -0.5),
        "final_norm_g": 1.0 + nrm(19, (D_MODEL,), 0.02),
    }


def reference(x_prompt, x_sample, state_conv, state_shift, state_wkv, norm_g, w_in, mu_shift,
              conv_w, w_dec2, w0, w_a2, a0, k_k, k_a, r_k, lnx_g, lnx_b, w_out, final_norm_g):
    dt = x_prompt.dtype
    hp, hs = x_prompt, x_sample
    conv_p, shift_p, wkv_p, conv_s, shift_s, wkv_s = [], [], [], [], [], []
    for l in range(DEPTH):
        params = (norm_g[l], w_in[l], mu_shift[l], conv_w[l], w_dec2[l], w0[l], w_a2[l], a0[l],
                  k_k[l], k_a[l], r_k[l], lnx_g[l], lnx_b[l], w_out[l])
        hp, cp, sp, wp = mixer_layer(
            hp, jnp.zeros((BATCH, CONV_K - 1, CONV_W), dt), jnp.zeros((BATCH, D_MODEL), dt),
            jnp.zeros((BATCH, N_HEADS_RWKV, HEAD_DIM, HEAD_DIM), state_wkv.dtype), *params)
        hs, cs, ss, ws = mixer_layer(hs, state_conv[l], state_shift[l], state_wkv[l], *params)
        conv_p.append(cp); shift_p.append(sp); wkv_p.append(wp)
        conv_s.append(cs); shift_s.append(ss); wkv_s.append(ws)
    y_prompt = rms_norm(hp, final_norm_g)
    y_sample = rms_norm(hs, final_norm_g)
    return (y_prompt, y_sample, jnp.stack(conv_p), jnp.stack(shift_p), jnp.stack(wkv_p),
            jnp.stack(conv_s), jnp.stack(shift_s), jnp.stack(wkv_s))
```

```python
import math
import os
from contextlib import ExitStack

import numpy as np
import concourse.bass as bass
import concourse.mybir as mybir
from concourse.bass_utils import run_bass_kernel_spmd

F32 = mybir.dt.float32
BF16 = mybir.dt.bfloat16
ALU = mybir.AluOpType
AF = mybir.ActivationFunctionType
AX = mybir.AxisListType

NCORES = 8
D = 1024
TP = 2048
TS = 128
TT = TP + TS
INC = 4224
NG = 33
TB = 256
NBP = TP // TB
C = 64
RMS_EPS = 1e-6
GN_EPS = 64e-5
DECAY_SCALE = math.exp(-0.5)
SAME_ENG_SYNC = True
SCRW = 11264

V_MU = 0
V_CW = 17
V_W0 = 29
V_A0 = 33
V_KK = 37
V_KA = 41
V_RK = 45
V_LG = 49
V_LB = 53
NV = 57


class Prog:
    ENGS = ("pe", "act", "dve", "pool", "sp")

    def __init__(self, nc, es):
        self.nc = nc
        self.es = es
        self.q = {e: [] for e in self.ENGS}
        self.cnt = {e: 0 for e in self.ENGS}
        self.sem = {e: es.enter_context(nc.semaphore("s_" + e)) for e in self.ENGS if e != "sp"}
        self.seen = {e: {} for e in self.ENGS}
        self.lastw = {}
        self.readers = {}
        self.dgrp = {}
        self.pe_ops = []
        self.pe_flag_idx = []
        self.dead = False
        self.nops = 0
        self.maxops = int(os.environ.get("MK_MAXOPS", "0"))
        self.log = []

    def _dsem(self, g):
        if g not in self.dgrp:
            self.dgrp[g] = [self.es.enter_context(self.nc.semaphore("d_" + g)), 0]
        return self.dgrp[g]

    def _wait(self, eng, tk):
        if self.dead:
            return
        kind, who, val = tk
        if kind == "e":
            if who == eng and (eng == "pe" or not SAME_ENG_SYNC):
                return
            if who == "pe":
                import bisect
                j = bisect.bisect_left(self.pe_flag_idx, val)
                if j < len(self.pe_flag_idx):
                    val = j + 1
                else:
                    self.pe_ops[-1]["inc"] = True
                    self.pe_flag_idx.append(len(self.pe_ops))
                    val = len(self.pe_flag_idx)
            if self.seen[eng].get(who, 0) >= val:
                return
            self.seen[eng][who] = val
            sem = self.sem[who]
            self.q[eng].append(lambda h, sem=sem, val=val: h.wait_ge(sem, val))
        else:
            ds = self._dsem(who)
            val = ds[1]
            key = ("d", who)
            if self.seen[eng].get(key, 0) >= val:
                return
            self.seen[eng][key] = val
            sem = ds[0]
            self.q[eng].append(lambda h, sem=sem, val=val: h.wait_ge(sem, val))

    def op(self, eng, fn, r=(), w=(), dma=None):
        if self.dead:
            return None
        self.nops += 1
        if self.maxops and self.nops > self.maxops:
            self.dead = True
            return None
        extra = [k + "_rd" for k in r if k.startswith("bank")]
        if extra:
            w = list(w) + extra
        deps = []
        for k in r:
            if k in self.lastw:
                deps.append(self.lastw[k])
        for k in w:
            if k in self.lastw:
                deps.append(self.lastw[k])
            deps.extend(self.readers.get(k, {}).values())
        for tk in deps:
            self._wait(eng, tk)
        if dma is None and eng == "pe":
            rec = {"fn": fn, "inc": False}
            self.pe_ops.append(rec)
            self.cnt[eng] = len(self.pe_ops)
            tk = ("e", eng, len(self.pe_ops))
            sem = self.sem[eng]

            def run(h, rec=rec, sem=sem):
                ins = rec["fn"](h)
                if rec["inc"]:
                    ins.then_inc(sem, 1)
            self.q[eng].append(run)
        elif dma is None:
            self.cnt[eng] += 1
            tk = ("e", eng, self.cnt[eng])
            sem = self.sem[eng]
            self.q[eng].append(lambda h, fn=fn, sem=sem: fn(h).then_inc(sem, 1))
        else:
            ds = self._dsem(dma)
            ds[1] += 16
            tk = ("d", dma, ds[1])
            sem = ds[0]
            self.q[eng].append(lambda h, fn=fn, sem=sem: fn(h).then_inc(sem, 16))
        for k in w:
            self.lastw[k] = tk
            self.readers[k] = {}
        for k in r:
            self.readers.setdefault(k, {})[(tk[0], tk[1])] = tk
        return tk

    def barrier_all_dma(self, eng):
        for g, ds in self.dgrp.items():
            if ds[1] > 0:
                self._wait(eng, ("d", g, ds[1]))


def build(debug=None, stop_at=None):
    nc = bass.Bass("TRN2", target_bir_lowering=False)
    es = ExitStack()

    def din(name, shape):
        return nc.dram_tensor(name, list(shape), F32, kind="ExternalInput").ap()

    def dout(name, shape):
        return nc.dram_tensor(name, list(shape), F32, kind="ExternalOutput").ap()

    x_d = din("x", [TT, D])
    sconv_d = din("sconv", [32, 512])
    sshift_d = din("sshift", [16, D])
    swkv_d = din("swkv", [128, 4096])
    win_d = din("w_in", [D, INC])
    wout_d = din("w_out", [D, D])
    wl_d = din("wl", [128, 512])
    vecs_d = din("vecs", [128, NV])
    gb_d = din("gb", [128, 2 * D])
    cst_d = din("cst", [128, 1024])

    y_d = dout("y", [TT, D])
    convp_d = dout("conv_p", [2, 512])
    shiftp_d = dout("shift_p", [1, D])
    wkvp_d = dout("wkv_p", [512, 64])
    convs_d = dout("conv_s", [32, 512])
    shifts_d = dout("shift_s", [16, D])
    wkvs_d = dout("wkv_s", [128, 4096])
    scr1_d = nc.dram_tensor("scr1", [128, 6 * 512], F32, kind="Internal").ap()
    scr2_d = nc.dram_tensor("scr2", [128, 512], F32, kind="Internal").ap()

    P = Prog(nc, es)

    def sb(name, shape, dt=F32):
        return es.enter_context(nc.sbuf_tensor("sb_" + name, list(shape), dt))

    def psum(name):
        return es.enter_context(nc.psum_tensor(name, [128, 512], F32))

    Wb = sb("Wb", [128, 8, INC], BF16)
    Wo = sb("Wo", [128, 8, D], BF16)
    gb = sb("gb", [128, 2 * D])
    cst = sb("cst", [128, 1024])
    vecs = sb("vecs", [128, NV])
    identb = sb("identb", [128, 128], BF16)
    onesb = sb("onesb", [128, 128], BF16)
    Wlb = sb("Wlb", [128, 512], BF16)
    LAb = sb("LAb", [128, TB], BF16)
    sqb = sb("sqb", [128, 2, TB], BF16)
    xt = [sb("xt%d" % i, [128, D]) for i in range(2)]
    xn32 = sb("xn32", [128, D])
    xnb = sb("xnb", [128, D], BF16)
    st1 = sb("st1", [128, 8])
    vx = sb("vx", [128, 24])
    xnT = sb("xnT", [128, 8, TB], BF16)
    PMW = sb("PMW", [128, 17 * TB])
    hs = sb("hs", [128, TB])
    bgs = sb("bgs", [128, TB])
    sgc = sb("sgc", [128, TB])
    zt = sb("zt", [128, TB])
    Uext = sb("Uext", [128, 4, TB + 2])
    Praw = [sb("Praw%d" % i, [128, TB + 1]) for i in range(2)]
    dtmp = sb("dtmp", [128, TB])
    carry = sb("carry", [128, 17])
    Yb = [sb("Y0", [128, 8, TB], BF16), sb("Y1", [128, 8, TB], BF16)]
    O = sb("O", [128, 4, TB])
    hbuf = sb("hbuf", [128, D])
    st2 = sb("st2", [128, 8])
    SCR = sb("SCR", [128, SCRW])
    ssT = sb("ssT", [128, 8, 16], BF16)
    Pfirst = sb("Pfirst", [128, 17, 16])
    scT = sb("scT", [128, 4, 32])
    tmS = sb("tmS", [128, 512])

    banks = [psum("ps%d" % i) for i in range(8)]

    ident = cst[:, 0:128]
    onesblk = cst[:, 128:256]
    MASK_A = cst[:, 256:512]
    MASK_Z = cst[:, 512:704]
    SCANM = cst[:, 704:960]
    IDENT2 = cst[:, 960:1024]

    def PM(g, T):
        return PMW[:, g * TB:g * TB + T]

    def vcol(c):
        return vecs[:, c:c + 1]

    class Carve:
        def __init__(self):
            self.off = 0

        def get(self, n):
            o = self.off
            self.off += n
            assert self.off <= SCRW, self.off
            return SCR[:, o:o + n]

    cv = Carve()

    def getb(n):
        return cv.get(n // 2).bitcast(BF16)

    R_sigw = cv.get(TB); R_a = cv.get(TB); R_L = cv.get(TB); R_t1 = cv.get(TB); R_t2 = cv.get(TB)
    R_enL = cv.get(TB); R_kkn = cv.get(TB); R_kmod = cv.get(TB); R_t3 = cv.get(TB)
    R_eL = [cv.get(TB) for _ in range(2)]
    R_bon = [cv.get(TB) for _ in range(2)]
    R_sg = [cv.get(TB) for _ in range(2)]
    P_t1 = cv.get(TB); P_t2 = cv.get(TB)
    r_end = cv.off
    R_t4 = cv.get(TB); R_t5 = cv.get(TB); R_t6 = cv.get(TB)
    NI = 4
    I_EF = [[cv.get(128) for _ in range(2)] for _ in range(NI)]
    SVb = [[cv.get(128) for _ in range(4)] for _ in range(4)]
    R_AR = [getb(2 * TB) for _ in range(2)]
    R_KB = [getb(2 * TB) for _ in range(2)]
    I_AA = [getb(256) for _ in range(NI)]
    I_Z = [[getb(256) for _ in range(2)] for _ in range(NI)]
    I_BKt = [getb(128) for _ in range(NI)]
    I_DW = [getb(64) for _ in range(NI)]
    I_BT = [getb(64) for _ in range(NI)]
    prompt_scr_end = cv.off
    cvs = Carve()
    cvs.off = r_end + 3 * TB
    S_blk = [cvs.get(1024) for _ in range(1)]
    S_tmp = [cvs.get(1024), SCR[:, 0:1024]]
    S_pmj = cvs.get(6 * 512)
    S_opm = cvs.get(512)
    S_sa = cvs.get(64)

    def dma(eng, out, in_, r, w, grp):
        P.op(eng, lambda h: h.dma_start(out=out, in_=in_), r=r, w=w, dma=grp)

    def mm(out, lhsT, rhs, r, w, start=True, stop=True):
        P.op("pe", lambda h: h.matmul(out, lhsT, rhs, start=start, stop=stop), r=r, w=w)

    def act(out, in_, func, r, w, bias=None, scale=None, accum_out=None):
        kw = {}
        if bias is not None:
            kw["bias"] = bias
        if scale is not None:
            kw["scale"] = scale
        if accum_out is not None:
            kw["accum_out"] = accum_out
        P.op("act", lambda h: h.activation(out, in_, func, **kw), r=r, w=w)

    def tt(eng, out, a, b, op, r, w):
        P.op(eng, lambda h: h.tensor_tensor(out, a, b, op), r=r, w=w)

    def ts(eng, out, a, s1, s2, op0, op1, r, w):
        if s2 is None:
            P.op(eng, lambda h: h.tensor_scalar(out, a, s1, None, op0), r=r, w=w)
        else:
            P.op(eng, lambda h: h.tensor_scalar(out, a, s1, s2, op0, op1), r=r, w=w)

    def stt(out, a, s, b, op0, op1, r, w):
        P.op("dve", lambda h: h.scalar_tensor_tensor(out, a, s, b, op0, op1), r=r, w=w)

    def cp(eng, out, in_, r, w):
        if eng == "act":
            P.op("act", lambda h: h.copy(out, in_), r=r, w=w)
        else:
            P.op(eng, lambda h: h.tensor_copy(out, in_), r=r, w=w)

    bank_rr = {"big": [0, [0, 1, 2]], "tr": [0, [3, 2]], "w": [0, [4, 5, 6, 7]]}

    def nbank(kind):
        st = bank_rr[kind]
        i = st[1][st[0] % len(st[1])]
        st[0] += 1
        return banks[i], "bank%d" % i

    dbg_outs = []

    def dbg(name, ap, keys, shape):
        if debug is None or name not in debug:
            return
        d = dout("dbg_" + name, shape)
        dma("sp", d, ap, r=keys, w=[], grp="dbg")
        dbg_outs.append("dbg_" + name)

    dma("sp", cst[:, :], cst_d[:, :], [], ["cst"], "c0")
    dma("sp", vecs[:, :], vecs_d[:, :], [], ["vecs"], "c0")
    dma("sp", gb[:, :], gb_d[:, :], [], ["gb"], "c0")
    dma("sp", tmS[:, :], wl_d[:, :], [], ["tmS"], "c0")
    cp("dve", identb[:, :], ident, ["cst"], ["identb"])
    cp("dve", onesb[:, :], onesblk, ["cst"], ["onesb"])
    cp("dve", Wlb[:, :], tmS[:, :], ["tmS"], ["Wlb"])
    ts("dve", vx[:, 0:4], vecs[:, V_KA:V_KA + 4], -1.0, None, ALU.mult, None, ["vecs"], ["vx"])
    ts("dve", vx[:, 4:21], vecs[:, V_MU:V_MU + 17], -1.0, 1.0, ALU.mult, ALU.add, ["vecs"], ["vx"])
    P.op("pool", lambda h: h.memset(carry[:, :], 0.0), r=[], w=["carry"])
    P.op("pool", lambda h: h.memset(Uext[:, :, 0:2], 0.0), r=[], w=["Uext0", "Uext1", "Uext2", "Uext3"])
    for hp in range(4):
        P.op("pool", lambda h, hp=hp: h.memset(SVb[hp][0][0:64, :], 0.0), r=[], w=["SV%d_0s" % hp])

    WCH = 128
    RORDER = [12, 0, 4, 8, 13, 1, 5, 9, 14, 2, 6, 10, 15, 3, 7, 11, 16]
    stg_bufs = [(O[:, :, :].rearrange("p a t -> p (a t)"), ["O0", "O1", "O2", "O3"]), (hbuf[:, :], ["hbuf"]),
                (xn32[:, :], ["xn32"])]
    g_order = [k * 4 + cg for cg in range(4) for k in range(4)] + [16 + r for r in RORDER]
    cast_eng = ["pool", "act", "dve"]
    stageq = []

    def mk_unit(wi, src_d, c0, dst, dkeys):
        def f():
            sbuf_, keys = stg_bufs[wi % 3]
            sv = sbuf_.rearrange("p (k c) -> p k c", k=8)
            dma("sp", sv, src_d[:, c0:c0 + WCH].rearrange("(k p) c -> p k c", p=128), [], keys, "stg%d" % (wi % 3))
            cp(cast_eng[wi % 3], dst[:, :, c0:c0 + WCH], sv, keys, dkeys)
        return f
    for wi, g in enumerate(g_order):
        stageq.append(mk_unit(wi, win_d, g * 128, Wb, ["Wb%d" % g]))
    for b_ in range(D // WCH):
        stageq.append(mk_unit(len(g_order) + b_, wout_d, b_ * WCH, Wo, ["Wo"]))
    STGK = []

    xt_i = [0]

    def norm_tile(tok, col, shift_rows=None, defer=None):
        xb = xt[xt_i[0] % 2]
        xk = "xt%d" % (xt_i[0] % 2)
        xt_i[0] += 1
        dma("sp", xb[:, :], x_d[tok:tok + 128, :], [], [xk], xk)
        act(xnb[:, :], xb[:, :], AF.Square, [xk], ["xnb", "st1"], accum_out=st1[:, 0:1])
        act(st1[:, 1:2], st1[:, 0:1], AF.Sqrt, ["st1"], ["st1"], bias=RMS_EPS, scale=1.0 / D)
        P.op("dve", lambda h: h.reciprocal(st1[:, 2:3], st1[:, 1:2]), r=["st1"], w=["st1"])
        stt(xnb[:, :], xb[:, :], st1[:, 2:3], gb[:, 0:D], ALU.mult, ALU.mult, [xk, "st1", "gb"], ["xnb"])
        if shift_rows is not None:
            stt(xn32[:, :], xb[:, :], st1[:, 2:3], gb[:, 0:D], ALU.mult, ALU.mult, [xk, "st1", "gb"], ["xn32"])
            shift_rows()
        if defer is not None:
            defer.append(lambda: norm_tile_b(col))
        else:
            norm_tile_b(col)

    def norm_tile_b(col):
        for half in range(2):
            bk, bkk = nbank("tr")
            for kk in range(4):
                k = half * 4 + kk
                mm(bk[:, kk * 128:(kk + 1) * 128], xnb[:, k * 128:(k + 1) * 128], identb[:, :],
                   ["xnb", "identb"], [bkk])
            cp("act" if half == 0 else "dve", xnT[:, half * 4:half * 4 + 4, col:col + 128],
               bk[:, :].rearrange("p (k t) -> p k t", k=4), [bkk], ["xnT"])

    def inproj(g, T):
        bk, bkk = nbank("big")
        for k in range(8):
            mm(bk[:, 0:T], Wb[:, k, g * 128:(g + 1) * 128], xnT[:, k, 0:T], ["Wb%d" % g, "xnT"], [bkk],
               start=(k == 0), stop=(k == 7))
        return bk, bkk

    def conv_branch(T, sample, yp=0, tick=lambda: None):
        Y = Yb[yp]
        for cg in range(4):
            uk = "Uext%d" % cg
            bk, bkk = inproj(cg, T)
            cp("act", hs[:, 0:T], bk[:, 0:T], [bkk], ["hs"])
            tick()
            bk, bkk = inproj(4 + cg, T)
            cp("act", bgs[:, 0:T], bk[:, 0:T], [bkk], ["bgs"])
            tick()
            bk, bkk = inproj(8 + cg, T)
            if not sample:
                tt("dve", Uext[:, cg, 2:2 + T], bk[:, 0:T], hs[:, 0:T], ALU.mult, [bkk, "hs"], [uk])
                u0, u1, u2 = Uext[:, cg, 0:T], Uext[:, cg, 1:T + 1], Uext[:, cg, 2:T + 2]
                zo = zt[:, 0:T]
            else:
                U3 = Uext[:, cg, 0:160].rearrange("p (b t) -> p b t", t=10)
                tt("dve", U3[:, :, 2:10], bk[:, 0:T].rearrange("p (b t) -> p b t", t=8),
                   hs[:, 0:T].rearrange("p (b t) -> p b t", t=8), ALU.mult, [bkk, "hs"], [uk])
                cp("pool", U3[:, :, 0:2], scT[:, cg, :].rearrange("p (b k) -> p b k", k=2), ["scT"], [uk])
                u0, u1, u2 = U3[:, :, 0:8], U3[:, :, 1:9], U3[:, :, 2:10]
                zo = zt[:, 0:T].rearrange("p (b t) -> p b t", t=8)
            tick()
            bk, bkk = inproj(12 + cg, T)
            act(sgc[:, 0:T], bk[:, 0:T], AF.Silu, [bkk], ["sgc"])
            tick()
            ts("dve", zo, u0, vcol(V_CW + 0 * 4 + cg), None, ALU.mult, None, [uk, "vecs"], ["zt"])
            stt(zo, u1, vcol(V_CW + 1 * 4 + cg), zo, ALU.mult, ALU.add, [uk, "vecs", "zt"], ["zt"])
            stt(zo, u2, vcol(V_CW + 2 * 4 + cg), zo, ALU.mult, ALU.add, [uk, "vecs", "zt"], ["zt"])
            tt("pool", zt[:, 0:T], zt[:, 0:T], bgs[:, 0:T], ALU.mult, ["zt", "bgs"], ["zt"])
            tt("pool", Y[:, cg, 0:T], zt[:, 0:T], sgc[:, 0:T], ALU.mult, ["zt", "sgc"], ["Y%d_%d" % (yp, cg)])

    def conv_carry(T):
        for cg in range(4):
            uk = "Uext%d" % cg
            cp("pool", Uext[:, cg, 0:2], Uext[:, cg, T:T + 2], [uk], [uk])

    def rwkv_proj(T, sample, tick=lambda n: None):
        for n, ri in enumerate(RORDER):
            tick(n)
            g = 16 + ri
            bk, bkk = inproj(g, T)
            pr = Praw[n % 2]
            pk = "Praw%d" % (n % 2)
            mu = vcol(V_MU + ri)
            omu = vx[:, 4 + ri:5 + ri]
            if not sample:
                ts("dve", pr[:, 1:T + 1], bk[:, 0:T], mu, None, ALU.mult, None, [bkk, "vecs"], [pk])
                cp("dve", pr[:, 0:1], carry[:, ri:ri + 1], ["carry"], [pk])
                stt(PM(ri, T), bk[:, 0:T], omu, pr[:, 0:T], ALU.mult, ALU.add, [bkk, pk, "vx"], ["PM%d" % ri] + STGK)
                cp("dve", carry[:, ri:ri + 1], pr[:, T:T + 1], [pk], ["carry"])
            else:
                act(pr[:, 1:T + 1], bk[:, 0:T], AF.Copy, [bkk, "vecs"], [pk], scale=mu)
                d3 = dtmp[:, 0:T].rearrange("p (b t) -> p b t", t=8)
                p3 = pr[:, 1:T + 1].rearrange("p (b t) -> p b t", t=8)
                cp("pool", d3[:, :, 1:8], p3[:, :, 0:7], [pk], ["dtmp"])
                act(d3[:, :, 0:1], Pfirst[:, ri, :].rearrange("p (b o) -> p b o", o=1), AF.Copy, ["Pfirst", "vecs"], ["dtmp"],
                    scale=mu)
                stt(PM(ri, T), bk[:, 0:T], omu, dtmp[:, 0:T], ALU.mult, ALU.add, [bkk, "dtmp", "vx"], ["PM%d" % ri] + STGK)
            if ri == 12:
                lora_prep(T)

    def lora_prep(T):
        act(LAb[0:64, 0:T], PMW[0:64, 12 * TB:12 * TB + T], AF.Tanh, ["PM12"], ["LA"])
        cp("dve", LAb[64:128, 0:T], PMW[64:128, 12 * TB:12 * TB + T], ["PM12"], ["LA"])

    def prep_steps(hp, T, sample, sb_, boff=0):
        rk = ["PM%d" % hp]
        kk_ = ["PM%d" % (4 + hp)]
        vk = ["PM%d" % (8 + hp)]
        gk = ["PM%d" % (13 + hp)]
        r_ = PM(hp, T); k_ = PM(4 + hp, T); v_ = PM(8 + hp, T); g_ = PM(13 + hp, T)
        eL = R_eL[sb_]; bon = R_bon[sb_][:, boff:]; sg = R_sg[sb_][:, boff:]
        eLk = "eL%d" % sb_; bonk = "bon%d" % sb_; sgk = "sg%d" % sb_; ARk = "AR%d" % sb_; KBk = "KB%d" % sb_
        AR = R_AR[sb_].rearrange("p (a t) -> p a t", a=2)
        KB = R_KB[sb_].rearrange("p (a t) -> p a t", a=2)
        def A():
            bk, bkk = nbank("tr")
            mm(bk[:, 0:T], Wlb[0:64, hp * 128:(hp + 1) * 128], LAb[0:64, 0:T], ["Wlb", "LA"], [bkk])
            act(R_sigw[:, 0:T], bk[:, 0:T], AF.Sigmoid, [bkk, "vecs"], ["sigw"], bias=vcol(V_W0 + hp))
            bk, bkk = nbank("tr")
            mm(bk[:, 0:T], Wlb[64:128, hp * 128:(hp + 1) * 128], LAb[64:128, 0:T], ["Wlb", "LA"], [bkk])
            act(R_a[:, 0:T], bk[:, 0:T], AF.Sigmoid, [bkk, "vecs"], ["a"], bias=vcol(V_A0 + hp))
            ts("dve", R_t1[:, 0:T], k_, vcol(V_KK + hp), None, ALU.mult, None, kk_ + ["vecs"], ["t1"])
            tt("pool", sqb[:, 0, 0:T], R_t1[:, 0:T], R_t1[:, 0:T], ALU.mult, ["t1"], ["sq0"])

        def B1():
            ts("dve", R_t5[:, 0:T], R_a[:, 0:T], vcol(V_KA + hp), vx[:, hp:hp + 1], ALU.mult, ALU.add, ["a", "vecs", "vx"], ["t5"])
            stt(R_kmod[:, 0:T], R_t5[:, 0:T], 1.0, k_, ALU.add, ALU.mult, ["t5"] + kk_, ["kmod"])
            act(R_t6[:, 0:T], R_sigw[:, 0:T], AF.Copy, ["sigw"], ["t6"], scale=-DECAY_SCALE)

        def B2():
            bk, bkk = nbank("tr")
            mm(bk[:, 0:T], onesb[:, :], sqb[:, 0, 0:T], ["onesb", "sq0"], [bkk])
            act(R_t2[:, 0:T], bk[:, 0:T], AF.Sqrt, [bkk], ["t2"])

        def C1():
            stt(sqb[:, 1, 0:T], r_, vcol(V_RK + hp), R_kmod[:, 0:T], ALU.mult, ALU.mult, rk + ["kmod", "vecs"], ["sq1"])

        def C2():
            ts("dve", R_t2[:, 0:T], R_t2[:, 0:T], 1e-12, None, ALU.max, None, ["t2"], ["t2"])
            P.op("dve", lambda h: h.reciprocal(R_t2[:, 0:T], R_t2[:, 0:T]), r=["t2"], w=["t2"])
            tt("pool", R_kkn[:, 0:T], R_t1[:, 0:T], R_t2[:, 0:T], ALU.mult, ["t1", "t2"], ["kkn"])
            tt("pool", R_t3[:, 0:T], R_kkn[:, 0:T], R_a[:, 0:T], ALU.mult, ["kkn", "a"], ["t3"])

        def Dd():
            P.op("dve", lambda h: h.tensor_tensor_scan(R_L[:, 0:T], SCANM[:, 0:T], R_t6[:, 0:T], 0.0, ALU.mult, ALU.add),
                 r=["cst", "t6"], w=["L"])
            tt("pool", R_t5[:, 0:T], R_L[:, 0:T], R_t6[:, 0:T], ALU.subtract, ["L", "t6"], ["t5"])
            act(eL[:, 0:T], R_L[:, 0:T], AF.Exp, ["L"], [eLk])
            act(R_enL[:, 0:T], R_L[:, 0:T], AF.Exp, ["L"], ["enL"], scale=-1.0)
            act(R_t5[:, 0:T], R_t5[:, 0:T], AF.Exp, ["t5"], ["t5"])

        def E():
            stt(AR[:, 0, 0:T], R_kkn[:, 0:T], -1.0, R_t5[:, 0:T], ALU.mult, ALU.mult, ["kkn", "t5"], [ARk])
            tt("pool", AR[:, 1, 0:T], r_, eL[:, 0:T], ALU.mult, rk + [eLk], [ARk])
            tt("pool", KB[:, 0, 0:T], R_kmod[:, 0:T], R_enL[:, 0:T], ALU.mult, ["kmod", "enL"], [KBk])
            tt("dve", KB[:, 1, 0:T], R_t3[:, 0:T], R_enL[:, 0:T], ALU.mult, ["t3", "enL"], [KBk])

        def F():
            bk, bkk = nbank("tr")
            mm(bk[:, 0:T], onesb[:, :], sqb[:, 1, 0:T], ["onesb", "sq1"], [bkk])
            tt("dve", bon[:, 0:T], bk[:, 0:T], v_, ALU.mult, [bkk] + vk, [bonk])
            act(sg[:, 0:T], g_, AF.Silu, gk, [sgk])
        if sample:
            return [A, B1, B2, C1, C2, F]
        return [A, B1, B2, C1, C2, Dd, E, F]

    def wkv_pre(hp, blk, T, sb_, tick):
        AR = R_AR[sb_].rearrange("p (a t) -> p a t", a=2)
        KB = R_KB[sb_].rearrange("p (a t) -> p a t", a=2)
        ARk = "AR%d" % sb_; KBk = "KB%d" % sb_; eLk = "eL%d" % sb_
        eL = R_eL[sb_]
        v_ = PM(8 + hp, T)
        vkey = "PM%d" % (8 + hp)
        nch = T // C
        inst = list(range(nch))
        cs = [slice(c * C, (c + 1) * C) for c in range(nch)]
        AAk = ["AA%d" % i for i in inst]
        HS = [slice(0, 64), slice(64, 128)]
        for i in inst:
            bk, bkk = nbank("w")
            for ps in HS:
                mm(bk[ps, 0:128].rearrange("p (a t) -> p a t", a=2), KB[ps, 1, cs[i]], AR[ps, :, cs[i]], [KBk, ARk], [bkk])
                mm(bk[ps, 128:256].rearrange("p (a t) -> p a t", a=2), KB[ps, 0, cs[i]], AR[ps, :, cs[i]], [KBk, ARk], [bkk])
            tt("dve", I_AA[i], bk[:, 0:256], MASK_A, ALU.mult, [bkk, "cst"], [AAk[i]])
            if i % 2 == 1:
                tick()
        zp = [0] * nch
        for i in inst:
            bk, bkk = nbank("w")
            for ps in HS:
                mm(bk[ps, 0:64], AR[ps, 0, cs[i]], identb[ps, ps], [ARk, "identb"], [bkk])
                mm(bk[ps, 64:192].rearrange("p (a t) -> p a t", a=2), AR[ps, 0, cs[i]], KB[ps, :, cs[i]], [ARk, KBk], [bkk])
            tt("dve", I_Z[i][0][:, 0:192], bk[:, 0:192], MASK_Z, ALU.mult, [bkk, "cst"], ["Z%d_0" % i])
            if i % 2 == 1:
                tick()
        for i in inst:
            wc = eL[:, i * C + C - 1:i * C + C]
            BKt = I_BKt[i].rearrange("p (a t) -> p a t", a=2)
            ts("dve", BKt[:, :, :], KB[:, :, cs[i]], wc, None, ALU.mult, None, [KBk, eLk], ["BKt%d" % i])
            act(I_DW[i], IDENT2, AF.Copy, ["cst", eLk], ["DW%d" % i], scale=wc)
            bk, bkk = nbank("w")
            for ps in HS:
                mm(bk[ps, 0:64], BKt[ps, 1, :], identb[ps, ps], ["BKt%d" % i, "identb"], [bkk])
            mm(bk[64:128, 64:192], v_[:, cs[i]], ident, [vkey, "cst"], [bkk])
            cp("act", I_BT[i], bk[:, 0:64], [bkk], ["BT%d" % i])
            cp("act", SVb[hp][i][64:128, :], bk[64:128, 64:192], [bkk], ["SV%d_%dv" % (hp, i)])
            if i % 2 == 1:
                tick()
        for lv in range(6):
            for i in inst:
                zi = I_Z[i][zp[i]]
                zo = I_Z[i][1 - zp[i]]
                zik = "Z%d_%d" % (i, zp[i])
                zok = "Z%d_%d" % (i, 1 - zp[i])
                bk, bkk = nbank("w")
                for ps in HS:
                    if lv == 0:
                        nat, natk = I_AA[i][ps, 0:64], AAk[i]
                    else:
                        nat, natk = zi[ps, 192:256], zik
                    mm(bk[ps, 0:128], identb[ps, ps], zi[ps, 0:128], ["identb", zik], [bkk], start=True, stop=False)
                    mm(bk[ps, 0:128], nat, zi[ps, 0:128], [natk, zik], [bkk], start=False, stop=(lv == 5))
                    if lv < 5:
                        mm(bk[ps, 128:192], nat, zi[ps, 128:192], [natk, zik], [bkk], start=False, stop=False)
                        mm(bk[ps, 192:256], zi[ps, 128:192], nat, [natk, zik], [bkk], start=False, stop=True)
                ncol = 256 if lv < 5 else 128
                cp("dve" if (i + lv) % 2 == 0 else "act", zo[:, 0:ncol], bk[:, 0:ncol], [bkk], [zok])
                zp[i] = 1 - zp[i]
            tick()
        for i in inst:
            Yk = "Z%d_%d" % (i, zp[i])
            Yt = I_Z[i][zp[i]]
            BKt = I_BKt[i].rearrange("p (a t) -> p a t", a=2)
            for hh in range(2):
                ps = HS[hh]
                bk, bkk = nbank("w")
                mm(bk[:, 0:64], Yt[ps, 0:128], I_BT[i][ps, :], [Yk, "BT%d" % i], [bkk], start=True, stop=False)
                mm(bk[:, 64:128], Yt[ps, 0:128], I_AA[i][ps, 64:128], [Yk, AAk[i]], [bkk], start=False, stop=False)
                mm(bk[0:64, 0:64], identb[ps, ps], I_DW[i][ps, :], ["identb", "DW%d" % i], [bkk], start=False, stop=False)
                mm(bk[0:64, 64:128], identb[ps, ps], AR[ps, 1, cs[i]], ["identb", ARk], [bkk], start=False, stop=True)
                mm(bk[64:128, 0:64], BKt[ps, 0, :], identb[ps, ps], ["BKt%d" % i, "identb"], [bkk], start=False, stop=False)
                mm(bk[64:128, 64:128], identb[ps, ps], I_AA[i][ps, 192:256], ["identb", AAk[i]], [bkk], start=False,
                   stop=True)
                cp("act" if hh == 0 else "dve", I_EF[i][hh], bk[:, 0:128], [bkk], ["EF%d_%d" % (i, hh)])
            tick()

    def chain_steps(hp, T, last_block):
        nch = T // C
        HS = [slice(0, 64), slice(64, 128)]
        cs = [slice(c * C, (c + 1) * C) for c in range(nch)]
        S = []

        def step(i):
            sv = SVb[hp][i].rearrange("p (h c) -> p h c", h=2)
            svn = SVb[hp][(i + 1) % 4]
            svk = ["SV%d_%ds" % (hp, i), "SV%d_%dv" % (hp, i)]
            bk, bkk = nbank("w")
            for hh in range(2):
                mm(bk[0:64, hh * 64:hh * 64 + 64], I_EF[i][hh][:, 0:64], sv[:, hh, :], ["EF%d_%d" % (i, hh)] + svk, [bkk])
            for hh in range(2):
                ps = HS[hh]
                mm(bk[ps, 128:192], sv[:, hh, :], I_EF[i][hh][:, 64:128], ["EF%d_%d" % (i, hh)] + svk, [bkk])
            cp("dve", svn[0:64, :], bk[0:64, 0:128], [bkk], ["SV%d_%ds" % (hp, (i + 1) % 4)])
            cp("act", O[:, hp, cs[i]], bk[:, 128:192], [bkk], ["O%d" % hp])
        for i in range(nch):
            S.append(lambda i=i: step(i))

        def fin():
            par = 0
            sv = SVb[hp][par].rearrange("p (h c) -> p h c", h=2)
            bk, bkk = nbank("w")
            for hh in range(2):
                mm(bk[0:64, hh * 64:hh * 64 + 64], sv[0:64, hh, :], ident[0:64, 0:64], ["SV%d_%ds" % (hp, par), "cst"],
                   [bkk])
            cp("dve", tmS[0:64, hp * 128:hp * 128 + 128], bk[0:64, 0:128], [bkk], ["tmS"])
            for hh in range(2):
                h_ = hp * 2 + hh
                dma("sp", wkvp_d[h_ * 64:(h_ + 1) * 64, :], tmS[0:64, hp * 128 + hh * 64:hp * 128 + hh * 64 + 64],
                    ["tmS"], [], "o_wkvp")
        if last_block:
            S.append(fin)
        return S

    def post_steps(hp, T, sb_, yp=0, boff=0):
        Y = Yb[yp]
        Oh = O[:, hp, 0:T]
        ok = "O%d" % hp
        bon = R_bon[sb_][:, boff:]; sg = R_sg[sb_][:, boff:]
        bonk = "bon%d" % sb_; sgk = "sg%d" % sb_

        def p0():
            bk, bkk = nbank("tr")
            mm(bk[:, 0:T], onesblk, Oh, ["cst", ok], [bkk])
            stt(P_t1[:, 0:T], bk[:, 0:T], -1.0 / 64, Oh, ALU.mult, ALU.add, [bkk, ok], ["pt1"])
            tt("pool", P_t2.bitcast(BF16)[:, 0:T], P_t1[:, 0:T], P_t1[:, 0:T], ALU.mult, ["pt1"], ["pt2"])

        def p1():
            bk, bkk = nbank("tr")
            mm(bk[:, 0:T], onesb[:, :], P_t2.bitcast(BF16)[:, 0:T], ["onesb", "pt2"], [bkk])
            act(P_t2[:, 0:T], bk[:, 0:T], AF.Sqrt, [bkk], ["pt2"], bias=GN_EPS, scale=1.0 / 64)
            P.op("dve", lambda h: h.reciprocal(P_t2[:, 0:T], P_t2[:, 0:T]), r=["pt2"], w=["pt2"])
            tt("dve", P_t1[:, 0:T], P_t1[:, 0:T], P_t2[:, 0:T], ALU.mult, ["pt1", "pt2"], ["pt1"])
            ts("dve", P_t1[:, 0:T], P_t1[:, 0:T], vcol(V_LG + hp), vcol(V_LB + hp), ALU.mult, ALU.add, ["pt1", "vecs"], ["pt1"])
            tt("pool", P_t1[:, 0:T], P_t1[:, 0:T], bon[:, 0:T], ALU.add, ["pt1", bonk], ["pt1"])
            tt("dve", Y[:, 4 + hp, 0:T], P_t1[:, 0:T], sg[:, 0:T], ALU.mult, ["pt1", sgk], ["Y%d_%d" % (yp, 4 + hp)])
        return [p0, p1]

    def post(hp, T, sb_, yp=0, boff=0):
        for st in post_steps(hp, T, sb_, yp, boff):
            st()

    def outproj(tok, col, yp=0):
        Y = Yb[yp]
        dma("sp", hbuf[:, :], x_d[tok:tok + 128, :], [], ["hbuf"], "hbuf")
        for half in range(2):
            bk, bkk = nbank("big")
            for g in range(8):
                mm(bk[:, :], Y[:, g, col:col + 128], Wo[:, g, half * 512:(half + 1) * 512], ["Y%d_%d" % (yp, g), "Wo"], [bkk],
                   start=(g == 0), stop=(g == 7))
            tt("dve", hbuf[:, half * 512:(half + 1) * 512], hbuf[:, half * 512:(half + 1) * 512], bk[:, :], ALU.add,
               ["hbuf", bkk], ["hbuf"])
        act(xnb[:, :], hbuf[:, :], AF.Square, ["hbuf"], ["xnb", "st2"], accum_out=st2[:, 0:1])
        act(st2[:, 1:2], st2[:, 0:1], AF.Sqrt, ["st2"], ["st2"], bias=RMS_EPS, scale=1.0 / D)
        P.op("dve", lambda h: h.reciprocal(st2[:, 2:3], st2[:, 1:2]), r=["st2"], w=["st2"])
        stt(hbuf[:, :], hbuf[:, :], st2[:, 2:3], gb[:, D:2 * D], ALU.mult, ALU.mult, ["hbuf", "st2", "gb"], ["hbuf"])
        dma("sp", y_d[tok:tok + 128, :], hbuf[:, :], ["hbuf"], [], "hbuf")

    def transpose_to_tm(src, srck, dstcols, T=128):
        bk, bkk = nbank("tr")
        mm(bk[:, 0:128], src, ident, srck + ["cst"], [bkk])
        cp("act", tmS[:, dstcols], bk[:, 0:128], [bkk], ["tmS"])

    def cut(n):
        if stop_at is not None and n == stop_at:
            if not P.dead:
                print("CUT", n, "nops", P.nops)
            P.dead = True

    cut(0)
    def norm_items(blk):
        tok0 = blk * TB
        items = []
        for t in range(2):
            def A(t=t):
                dl = []
                if blk == NBP - 1 and t == 1:
                    def srow():
                        dma("sp", shiftp_d[0:1, :], xn32[127:128, :], ["xn32"], [], "o_misc")
                    norm_tile(tok0 + t * 128, t * 128, srow, defer=dl)
                else:
                    norm_tile(tok0 + t * 128, t * 128, defer=dl)
                pendB.append(dl[0])

            def B():
                pendB.pop(0)()
            items += [A, B]
        return items
    pendB = []

    def norm_block(blk):
        for it in norm_items(blk):
            it()

    norm_block(0)
    for _ in range(3):
        stageq.pop(0)()
    tailq = []
    for blk in range(NBP):
        T = TB
        tok0 = blk * TB
        last = blk == NBP - 1
        yp = blk % 2

        tcnt = [0]

        def ttick(tailq=tailq, tcnt=tcnt):
            if stageq:
                stageq.pop(0)()
            tcnt[0] += 1
            if tailq and tcnt[0] % 2 == 1:
                tailq.pop(0)()
        cut(1 if blk == 0 else -1)
        conv_branch(T, False, yp, ttick)
        cut(2 if blk == 0 else -1)
        if last:
            for cg in range(4):
                transpose_to_tm(Uext[:, cg, 2 + 128:2 + 256], ["Uext%d" % cg], slice(cg * 128, cg * 128 + 128))
            dma("sp", convp_d[:, :], tmS[126:128, :], ["tmS"], [], "o_misc")
        conv_carry(T)
        p0 = prep_steps(0, T, False, 0)

        def rtick(n, tailq=tailq, p0=p0):
            if stageq:
                stageq.pop(0)()
            if tailq:
                tailq.pop(0)()
            elif n >= 6 and p0:
                p0.pop(0)()
        rwkv_proj(T, False, rtick)
        while stageq:
            stageq.pop(0)()
        while tailq:
            tailq.pop(0)()
        cut(3 if blk == 0 else -1)
        while p0:
            p0.pop(0)()
        pend_chain, pend_post = [], []
        for hp in range(4):
            if hp < 3:
                other = prep_steps(hp + 1, T, False, (hp + 1) % 2)
            elif not last:
                other = norm_items(blk + 1)
            else:
                other = []
            q = []
            oth = list(other)
            last_item = [oth.pop()] if oth else []
            for c in pend_chain:
                if oth:
                    q.append(oth.pop(0))
                q.append(c)
            posts = list(pend_post)
            while oth or posts:
                if oth:
                    q.append(oth.pop(0))
                if posts and (len(oth) <= 1 or len(q) >= 11):
                    q.append(posts.pop(0))
                    if oth:
                        q.append(oth.pop(0))
            q += last_item

            tk_n = [0]
            sparse = (hp == 0)

            def tick(q=q, tk_n=tk_n, sparse=sparse):
                tk_n[0] += 1
                if q and (not sparse or tk_n[0] % 2 == 1):
                    q.pop(0)()
            wkv_pre(hp, blk, T, hp % 2, tick)
            while q:
                q.pop(0)()
            pend_chain = chain_steps(hp, T, last)
            pend_post = post_steps(hp, T, hp % 2, yp)
        tailq.extend(pend_chain + pend_post)
        for t in range(2):
            tailq.append(lambda t=t, tok0=tok0, yp=yp: outproj(tok0 + t * 128, t * 128, yp))
        cut(7 if blk == 0 else -1)
    cut(8)

    def global_barrier():
        for e in ("pe", "act", "dve", "pool", "sp"):
            for f in ("pe", "act", "dve", "pool"):
                if f != e and P.cnt[f] > 0:
                    P._wait(e, ("e", f, P.cnt[f]))
            P.barrier_all_dma(e)

    def tkS(n=None):
        if tailq:
            tailq.pop(0)()
    T = TS
    tok0 = TP
    dma("sp", xn32[0:16, :], sshift_d[:, :], [], ["xn32"], "c1")
    cp("pool", xnb[0:16, :], xn32[0:16, :], ["xn32"], ["xnb"])
    for half in range(2):
        bk, bkk = nbank("tr")
        for kk in range(4):
            k = half * 4 + kk
            mm(bk[:, kk * 16:(kk + 1) * 16], xnb[0:16, k * 128:(k + 1) * 128], identb[0:16, 0:16], ["xnb", "identb"], [bkk])
        cp("act", ssT[:, half * 4:half * 4 + 4, :], bk[:, 0:64].rearrange("p (k t) -> p k t", k=4), [bkk], ["ssT"])
    for ri in range(17):
        g = 16 + ri
        bk, bkk = nbank("big")
        for k in range(8):
            mm(bk[:, 0:16], Wb[:, k, g * 128:(g + 1) * 128], ssT[:, k, :], ["Wb%d" % g, "ssT"], [bkk], start=(k == 0),
               stop=(k == 7))
        cp("act", Pfirst[:, ri, :], bk[:, 0:16], [bkk], ["Pfirst"])
    dma("sp", xn32[0:32, 0:512], sconv_d[:, :], ["xnb"], ["xn32"], "c1")
    for cg in range(4):
        bk, bkk = nbank("tr")
        mm(bk[:, 0:32], xn32[0:32, cg * 128:(cg + 1) * 128], ident[0:32, 0:32], ["xn32", "cst"], [bkk])
        cp("act", scT[:, cg, :], bk[:, 0:32], [bkk], ["scT"])

    def srow_s():
        for b in range(16):
            dma("sp", shifts_d[b:b + 1, :], xn32[b * 8 + 7:b * 8 + 8, :], ["xn32"], [], "o_misc")
    norm_tile(tok0, 0, srow_s)
    conv_branch(T, True, 0, tkS)
    for cg in range(4):
        cp("pool", dtmp[:, 0:128].rearrange("p (b t) -> p b t", t=8),
           Uext[:, cg, 0:160].rearrange("p (b t) -> p b t", t=10)[:, :, 2:10], ["Uext%d" % cg], ["dtmp"])
        transpose_to_tm(dtmp[:, 0:128], ["dtmp"], slice(cg * 128, cg * 128 + 128))
    for b in range(16):
        dma("sp", convs_d[2 * b:2 * b + 2, :], tmS[b * 8 + 6:b * 8 + 8, :], ["tmS"], [], "o_misc")
    rwkv_proj(T, True, tkS)
    while tailq:
        tailq.pop(0)()
    PMJ = S_pmj.rearrange("p (t q j) -> p t q j", t=8, q=6)
    for hp in range(4):
        for st in prep_steps(hp, T, True, hp // 2, (hp % 2) * 128):
            st()
        act(R_eL[0][:, 0:T], R_sigw[:, 0:T], AF.Exp, ["sigw"], ["eL0"], scale=-DECAY_SCALE)
        srcs = [(PM(hp, T), ["PM%d" % hp]), (R_eL[0][:, 0:T], ["eL0"]), (R_kmod[:, 0:T], ["kmod"]),
                (PM(8 + hp, T), ["PM%d" % (8 + hp)]), (R_kkn[:, 0:T], ["kkn"]), (R_t3[:, 0:T], ["t3"])]
        for qi, (src, sk) in enumerate(srcs):
            sl = (hp * 6 + qi) % 4
            tsl = tmS[:, sl * 128:(sl + 1) * 128]
            bk, bkk = nbank("tr")
            mm(bk[:, 0:128], src, ident, sk + ["cst"], [bkk])
            cp("act" if qi % 2 == 0 else "dve", tsl, bk[:, 0:128], [bkk], ["tmS%d" % sl, "tmS"])
            dst = scr1_d[:, hp * 768:(hp + 1) * 768].rearrange("r (h c) -> r h c", h=2)[:, :, qi * 64:(qi + 1) * 64]
            dma("sp" if qi % 2 == 0 else "act", dst, tsl.rearrange("r (h j) -> r h j", h=2), ["tmS%d" % sl], [], "scr1w")
    global_barrier()
    for b in range(16):
        src = scr1_d[b * 8:(b + 1) * 8, :].rearrange("t (h c) -> h t c", h=8)
        dma("sp" if b % 2 == 0 else "act", S_pmj[b * 8:(b + 1) * 8, :].rearrange("p (t c) -> p t c", t=8), src, [], ["pmj"], "pmj")
    Opm = S_opm.rearrange("p (t i) -> p t i", t=8)
    for ib in range(4):
        Sb = S_blk[0]
        sk = "Sblk0"
        tmpb = S_tmp[0]
        tmpc = S_tmp[1]
        dma("sp", Sb, swkv_d[:, ib * 1024:(ib + 1) * 1024], [], [sk], sk)
        S3 = Sb.rearrange("p (i j) -> p i j", j=64)
        T3 = tmpb.rearrange("p (i j) -> p i j", j=64)
        U3 = tmpc.rearrange("p (i j) -> p i j", j=64)

        def bj(q, t):
            return PMJ[:, t, q, :].unsqueeze(1).to_broadcast([128, 16, 64])

        def bi(ap2):
            return ap2.unsqueeze(2).to_broadcast([128, 16, 64])
        for t in range(8):
            tt("dve", T3, S3, bj(4, t), ALU.mult, [sk, "pmj"], ["tmpb"])
            P.op("dve", lambda h, T3=T3: h.tensor_reduce(S_sa[:, 0:16], T3, AX.X, ALU.add, negate=True), r=["tmpb"], w=["sa"])
            tt("pool", U3, bi(PMJ[:, t, 3, ib * 16:(ib + 1) * 16]), bj(2, t), ALU.mult, ["pmj"], ["tmpc"])
            tt("dve", S3, S3, bj(1, t), ALU.mult, [sk, "pmj"], [sk])
            tt("dve", T3, bi(S_sa[:, 0:16]), bj(5, t), ALU.mult, ["sa", "pmj"], ["tmpb"])
            tt("dve", S3, S3, T3, ALU.add, [sk, "tmpb"], [sk])
            tt("dve", S3, S3, U3, ALU.add, [sk, "tmpc"], [sk])
            tt("dve", T3, S3, bj(0, t), ALU.mult, [sk, "pmj"], ["tmpb"])
            P.op("dve", lambda h, T3=T3, t=t, ib=ib: h.tensor_reduce(Opm[:, t, ib * 16:(ib + 1) * 16], T3, AX.X, ALU.add),
                 r=["tmpb"], w=["opm"])
        dma("sp", wkvs_d[:, ib * 1024:(ib + 1) * 1024], Sb, [sk], [], sk)
    for b in range(16):
        dst = scr2_d[b * 8:(b + 1) * 8, :].rearrange("t (h i) -> h t i", h=8)
        dma("sp" if b % 2 == 0 else "act", dst, Opm[b * 8:(b + 1) * 8, :, :], ["opm"], [], "scr2w")
    global_barrier()
    dma("sp", tmS[:, :], scr2_d[:, :], [], ["tmS"], "c1")
    for hp in range(4):
        bk, bkk = nbank("tr")
        mm(bk[:, 0:128], tmS[:, hp * 128:(hp + 1) * 128], ident, ["tmS", "cst"], [bkk])
        cp("act", O[:, hp, 0:128], bk[:, 0:128], [bkk], ["O%d" % hp])
    for hp in range(4):
        post(hp, T, hp // 2, 0, (hp % 2) * 128)
    outproj(tok0, 0)

    P.dead = False
    for e in ("sp",):
        for f in ("pe", "act", "dve", "pool"):
            P._wait(e, ("e", f, P.cnt[f]))
        P.barrier_all_dma(e)

    with nc.Block() as block:
        @block.sync
        def _(e):
            for t in P.q["sp"]:
                t(e)

        @block.tensor
        def _(e):
            for t in P.q["pe"]:
                t(e)

        @block.scalar
        def _(e):
            for t in P.q["act"]:
                t(e)

        @block.vector
        def _(e):
            for t in P.q["dve"]:
                t(e)

        @block.gpsimd
        def _(e):
            for t in P.q["pool"]:
                t(e)
    es.close()
    return nc, dbg_outs


def _consts():
    c = np.zeros((128, 1024), np.float32)
    p = np.arange(128)
    c[:, 0:128] = np.eye(128, dtype=np.float32)
    c[:, 128:256] = (p[:, None] // 64 == p[None, :] // 64).astype(np.float32)
    s = (p % 64)[:, None]
    t = np.arange(64)[None, :]
    strict = (s < t).astype(np.float32)
    incl = (s <= t).astype(np.float32)
    c[:, 256:512] = np.concatenate([strict, incl, strict, incl], axis=1)
    lower = (s > t).astype(np.float32)
    c[:, 512:704] = np.concatenate([np.ones((128, 64), np.float32), lower, lower], axis=1)
    c[:, 704:960] = np.tile((np.arange(256) % 64 != 0).astype(np.float32)[None, :], (128, 1))
    c[:, 960:1024] = (s == t).astype(np.float32)
    return c


def _prep_inputs(inp):
    f = lambda a: np.ascontiguousarray(np.asarray(a, dtype=np.float32))
    xp = f(inp["x_prompt"]); xs = f(inp["x_sample"])
    sc = f(inp["state_conv"])[0]; ss = f(inp["state_shift"])[0]; sw = f(inp["state_wkv"])[0]
    colv = lambda v, n: f(v).reshape(n, 128).T
    vecs = np.zeros((128, NV), np.float32)
    vecs[:, V_MU:V_MU + 17] = colv(inp["mu_shift"][0], 17)
    cw = f(inp["conv_w"])[0]
    for k in range(3):
        vecs[:, V_CW + k * 4:V_CW + k * 4 + 4] = colv(cw[k], 4)
    vecs[:, V_W0:V_W0 + 4] = colv(inp["w0"][0], 4)
    vecs[:, V_A0:V_A0 + 4] = colv(inp["a0"][0], 4)
    vecs[:, V_KK:V_KK + 4] = colv(inp["k_k"][0], 4)
    vecs[:, V_KA:V_KA + 4] = colv(inp["k_a"][0], 4)
    vecs[:, V_RK:V_RK + 4] = colv(f(inp["r_k"])[0].reshape(512), 4)
    vecs[:, V_LG:V_LG + 4] = colv(inp["lnx_g"][0], 4)
    vecs[:, V_LB:V_LB + 4] = colv(inp["lnx_b"][0], 4)
    gbv = np.concatenate([f(inp["norm_g"])[0], f(inp["final_norm_g"])])[None, :]
    gbv = np.ascontiguousarray(np.broadcast_to(gbv, (128, 2 * D)))
    wl = np.ascontiguousarray(np.concatenate([f(inp["w_dec2"])[0], f(inp["w_a2"])[0]], axis=0))
    cst = _consts()
    win = f(inp["w_in"])[0]; wout = f(inp["w_out"])[0]
    maps = []
    for c in range(NCORES):
        xa = np.ascontiguousarray(np.concatenate([xp[c], xs[16 * c:16 * c + 16].reshape(128, D)], axis=0))
        maps.append({
            "x": xa,
            "sconv": np.ascontiguousarray(sc[16 * c:16 * c + 16].reshape(32, 512)),
            "sshift": np.ascontiguousarray(ss[16 * c:16 * c + 16]),
            "swkv": np.ascontiguousarray(sw[16 * c:16 * c + 16].reshape(128, 4096)),
            "w_in": win, "w_out": wout, "wl": wl, "vecs": vecs, "gb": gbv, "cst": cst,
        })
    return maps


_NC_CACHE = {}


def kernel(**inputs):
    if "nc" not in _NC_CACHE:
        _NC_CACHE["nc"] = build()[0]
    nc = _NC_CACHE["nc"]
    maps = _prep_inputs(inputs)
    res = run_bass_kernel_spmd(nc, maps, core_ids=list(range(NCORES)))
    R = res.results
    y_p = np.stack([R[c]["y"][0:TP] for c in range(NCORES)], axis=0)
    y_s = np.concatenate([R[c]["y"][TP:TT].reshape(16, 8, D) for c in range(NCORES)], axis=0)
    conv_p = np.stack([R[c]["conv_p"] for c in range(NCORES)], axis=0)[None]
    shift_p = np.stack([R[c]["shift_p"][0] for c in range(NCORES)], axis=0)[None]
    wkv_p = np.stack([R[c]["wkv_p"].reshape(8, 64, 64) for c in range(NCORES)], axis=0)[None]
    conv_s = np.concatenate([R[c]["conv_s"].reshape(16, 2, 512) for c in range(NCORES)], axis=0)[None]
    shift_s = np.concatenate([R[c]["shift_s"] for c in range(NCORES)], axis=0)[None]
    wkv_s = np.concatenate([R[c]["wkv_s"].reshape(16, 8, 64, 64) for c in range(NCORES)], axis=0)[None]
    return tuple(np.ascontiguousarray(a.astype(np.float32)) for a in
                 (y_p, y_s, conv_p, shift_p, wkv_p, conv_s, shift_s, wkv_s))
```

```python
import math
import os
from contextlib import ExitStack

import numpy as np
import concourse.bass as bass
import concourse.mybir as mybir
from concourse.bass_utils import run_bass_kernel_spmd

F32 = mybir.dt.float32
BF16 = mybir.dt.bfloat16
ALU = mybir.AluOpType
AF = mybir.ActivationFunctionType
AX = mybir.AxisListType

NCORES = 8
D = 1024
TP = 2048
TS = 128
TT = TP + TS
INC = 4224
NG = 33
TB = 256
NBP = TP // TB
C = 64
RMS_EPS = 1e-6
GN_EPS = 64e-5
DECAY_SCALE = math.exp(-0.5)
SAME_ENG_SYNC = True
SCRW = 11264

V_MU = 0
V_CW = 17
V_W0 = 29
V_A0 = 33
V_KK = 37
V_KA = 41
V_RK = 45
V_LG = 49
V_LB = 53
NV = 57


class Prog:
    ENGS = ("pe", "act", "dve", "pool", "sp")

    def __init__(self, nc, es):
        self.nc = nc
        self.es = es
        self.q = {e: [] for e in self.ENGS}
        self.cnt = {e: 0 for e in self.ENGS}
        self.sem = {e: es.enter_context(nc.semaphore("s_" + e)) for e in self.ENGS if e != "sp"}
        self.seen = {e: {} for e in self.ENGS}
        self.lastw = {}
        self.readers = {}
        self.dgrp = {}
        self.pend = {e: [] for e in self.ENGS}
        self.pe_ops = []
        self.pe_flag_idx = []
        self.dead = False
        self.nops = 0
        self.maxops = int(os.environ.get("MK_MAXOPS", "0"))
        self.log = []

    def _dsem(self, g):
        if g not in self.dgrp:
            self.dgrp[g] = [self.es.enter_context(self.nc.semaphore("d_" + g)), 0]
        return self.dgrp[g]

    def _wait(self, eng, tk):
        if self.dead:
            return
        kind, who, val = tk
        if kind == "e":
            if who == eng and (eng == "pe" or not SAME_ENG_SYNC):
                return
            if who == "pe":
                import bisect
                j = bisect.bisect_left(self.pe_flag_idx, val)
                if j < len(self.pe_flag_idx):
                    val = j + 1
                else:
                    self.pe_ops[-1]["inc"] = True
                    self.pe_flag_idx.append(len(self.pe_ops))
                    val = len(self.pe_flag_idx)
            if self.seen[eng].get(who, 0) >= val:
                return
            self.seen[eng][who] = val
            sem = self.sem[who]
            self.pend[eng].append((sem, val))
        else:
            ds = self._dsem(who)
            val = ds[1]
            key = ("d", who)
            if self.seen[eng].get(key, 0) >= val:
                return
            self.seen[eng][key] = val
            sem = ds[0]
            self.pend[eng].append((sem, val))

    def flush(self, eng, keep_last=False):
        pend = self.pend[eng]
        last = pend.pop() if (keep_last and pend) else None
        for sem, val in pend:
            self.q[eng].append(lambda h, sem=sem, val=val: h.wait_ge(sem, val))
        self.pend[eng] = []
        return last

    def op(self, eng, fn, r=(), w=(), dma=None):
        if self.dead:
            return None
        self.nops += 1
        if self.maxops and self.nops > self.maxops:
            self.dead = True
            return None
        extra = [k + "_rd" for k in r if k.startswith("bank")]
        if extra:
            w = list(w) + extra
        deps = []
        for k in r:
            if k in self.lastw:
                deps.append(self.lastw[k])
        for k in w:
            if k in self.lastw:
                deps.append(self.lastw[k])
            deps.extend(self.readers.get(k, {}).values())
        for tk in deps:
            self._wait(eng, tk)
        att = self.flush(eng, keep_last=(dma is None))
        if att is not None:
            fn0, (asem, aval) = fn, att
            fn = lambda h, fn0=fn0, asem=asem, aval=aval: fn0(h)._wait_ge(asem, aval)
        if dma is None and eng == "pe":
            rec = {"fn": fn, "inc": False}
            self.pe_ops.append(rec)
            self.cnt[eng] = len(self.pe_ops)
            tk = ("e", eng, len(self.pe_ops))
            sem = self.sem[eng]

            def run(h, rec=rec, sem=sem):
                ins = rec["fn"](h)
                if rec["inc"]:
                    ins.then_inc(sem, 1)
            self.q[eng].append(run)
        elif dma is None:
            self.cnt[eng] += 1
            tk = ("e", eng, self.cnt[eng])
            sem = self.sem[eng]
            self.q[eng].append(lambda h, fn=fn, sem=sem: fn(h).then_inc(sem, 1))
        else:
            ds = self._dsem(dma)
            ds[1] += 16
            tk = ("d", dma, ds[1])
            sem = ds[0]
            self.q[eng].append(lambda h, fn=fn, sem=sem: fn(h).then_inc(sem, 16))
        for k in w:
            self.lastw[k] = tk
            self.readers[k] = {}
        for k in r:
            self.readers.setdefault(k, {})[(tk[0], tk[1])] = tk
        return tk

    def barrier_all_dma(self, eng):
        for g, ds in self.dgrp.items():
            if ds[1] > 0:
                self._wait(eng, ("d", g, ds[1]))


def build(debug=None, stop_at=None):
    nc = bass.Bass("TRN2", target_bir_lowering=False)
    es = ExitStack()

    def din(name, shape):
        return nc.dram_tensor(name, list(shape), F32, kind="ExternalInput").ap()

    def dout(name, shape):
        return nc.dram_tensor(name, list(shape), F32, kind="ExternalOutput").ap()

    x_d = din("x", [TT, D])
    sconv_d = din("sconv", [32, 512])
    sshift_d = din("sshift", [16, D])
    swkv_d = din("swkv", [128, 4096])
    win_d = din("w_in", [D, INC])
    wout_d = din("w_out", [D, D])
    wl_d = din("wl", [128, 512])
    vecs_d = din("vecs", [128, NV])
    gb_d = din("gb", [128, 2 * D])
    cst_d = din("cst", [128, 1024])

    y_d = dout("y", [TT, D])
    convp_d = dout("conv_p", [2, 512])
    shiftp_d = dout("shift_p", [1, D])
    wkvp_d = dout("wkv_p", [512, 64])
    convs_d = dout("conv_s", [32, 512])
    shifts_d = dout("shift_s", [16, D])
    wkvs_d = dout("wkv_s", [128, 4096])
    scr1_d = nc.dram_tensor("scr1", [128, 6 * 512], F32, kind="Internal").ap()
    scr2_d = nc.dram_tensor("scr2", [128, 512], F32, kind="Internal").ap()

    P = Prog(nc, es)

    def sb(name, shape, dt=F32):
        return es.enter_context(nc.sbuf_tensor("sb_" + name, list(shape), dt))

    def psum(name):
        return es.enter_context(nc.psum_tensor(name, [128, 512], F32))

    Wb = sb("Wb", [128, 8, INC], BF16)
    Wo = sb("Wo", [128, 8, D], BF16)
    gb = sb("gb", [128, 2 * D])
    cst = sb("cst", [128, 1024])
    vecs = sb("vecs", [128, NV])
    identb = sb("identb", [128, 128], BF16)
    onesb = sb("onesb", [128, 128], BF16)
    Wlb = sb("Wlb", [128, 512], BF16)
    LAb = sb("LAb", [128, TB], BF16)
    sqb = sb("sqb", [128, 2, TB], BF16)
    xt = [sb("xt%d" % i, [128, D]) for i in range(2)]
    xn32 = sb("xn32", [128, D])
    xnb = sb("xnb", [128, D], BF16)
    st1 = sb("st1", [128, 8])
    vx = sb("vx", [128, 24])
    xnT = sb("xnT", [128, 8, TB], BF16)
    PMW = sb("PMW", [128, 17 * TB])
    hs = sb("hs", [128, TB])
    bgs = sb("bgs", [128, TB])
    sgc = sb("sgc", [128, TB])
    zt = sb("zt", [128, TB])
    Uext = sb("Uext", [128, 4, TB + 2])
    Praw = [sb("Praw%d" % i, [128, TB + 1]) for i in range(2)]
    dtmp = sb("dtmp", [128, TB])
    carry = sb("carry", [128, 17])
    Yb = [sb("Y0", [128, 8, TB], BF16), sb("Y1", [128, 8, TB], BF16)]
    O = sb("O", [128, 4, TB])
    hbuf = sb("hbuf", [128, D])
    st2 = sb("st2", [128, 8])
    SCR = sb("SCR", [128, SCRW])
    ssT = sb("ssT", [128, 8, 16], BF16)
    Pfirst = sb("Pfirst", [128, 17, 16])
    scT = sb("scT", [128, 4, 32])
    tmS = sb("tmS", [128, 512])

    banks = [psum("ps%d" % i) for i in range(8)]

    ident = cst[:, 0:128]
    onesblk = cst[:, 128:256]
    MASK_A = cst[:, 256:512]
    MASK_Z = cst[:, 512:704]
    SCANM = cst[:, 704:960]
    IDENT2 = cst[:, 960:1024]

    def PM(g, T):
        return PMW[:, g * TB:g * TB + T]

    def vcol(c):
        return vecs[:, c:c + 1]

    class Carve:
        def __init__(self):
            self.off = 0

        def get(self, n):
            o = self.off
            self.off += n
            assert self.off <= SCRW, self.off
            return SCR[:, o:o + n]

    cv = Carve()

    def getb(n):
        return cv.get(n // 2).bitcast(BF16)

    R_sigw = cv.get(TB); R_a = cv.get(TB); R_L = cv.get(TB); R_t1 = cv.get(TB); R_t2 = cv.get(TB)
    R_enL = cv.get(TB); R_kkn = cv.get(TB); R_kmod = cv.get(TB); R_t3 = cv.get(TB)
    R_eL = [cv.get(TB) for _ in range(2)]
    R_bon = [cv.get(TB) for _ in range(2)]
    R_sg = [cv.get(TB) for _ in range(2)]
    P_t1 = cv.get(TB); P_t2 = cv.get(TB)
    r_end = cv.off
    R_t4 = cv.get(TB); R_t5 = cv.get(TB); R_t6 = cv.get(TB)
    NI = 4
    I_EF = [[cv.get(128) for _ in range(2)] for _ in range(NI)]
    SVb = [[cv.get(128) for _ in range(4)] for _ in range(4)]
    R_AR = [getb(2 * TB) for _ in range(2)]
    R_KB = [getb(2 * TB) for _ in range(2)]
    I_AA = [getb(256) for _ in range(NI)]
    I_Z = [[getb(256) for _ in range(2)] for _ in range(NI)]
    I_BKt = [getb(128) for _ in range(NI)]
    I_DW = [getb(64) for _ in range(NI)]
    I_BT = [getb(64) for _ in range(NI)]
    prompt_scr_end = cv.off
    cvs = Carve()
    cvs.off = r_end + 3 * TB
    S_blk = [cvs.get(1024) for _ in range(1)]
    S_tmp = [cvs.get(1024), SCR[:, 0:1024]]
    S_pmj = cvs.get(6 * 512)
    S_opm = cvs.get(512)
    S_sa = cvs.get(64)

    def dma(eng, out, in_, r, w, grp):
        P.op(eng, lambda h: h.dma_start(out=out, in_=in_), r=r, w=w, dma=grp)

    def mm(out, lhsT, rhs, r, w, start=True, stop=True):
        P.op("pe", lambda h: h.matmul(out, lhsT, rhs, start=start, stop=stop), r=r, w=w)

    def act(out, in_, func, r, w, bias=None, scale=None, accum_out=None):
        kw = {}
        if bias is not None:
            kw["bias"] = bias
        if scale is not None:
            kw["scale"] = scale
        if accum_out is not None:
            kw["accum_out"] = accum_out
        P.op("act", lambda h: h.activation(out, in_, func, **kw), r=r, w=w)

    def tt(eng, out, a, b, op, r, w):
        P.op(eng, lambda h: h.tensor_tensor(out, a, b, op), r=r, w=w)

    def ts(eng, out, a, s1, s2, op0, op1, r, w):
        if s2 is None:
            P.op(eng, lambda h: h.tensor_scalar(out, a, s1, None, op0), r=r, w=w)
        else:
            P.op(eng, lambda h: h.tensor_scalar(out, a, s1, s2, op0, op1), r=r, w=w)

    def stt(out, a, s, b, op0, op1, r, w):
        P.op("dve", lambda h: h.scalar_tensor_tensor(out, a, s, b, op0, op1), r=r, w=w)

    def cp(eng, out, in_, r, w):
        if eng == "act":
            P.op("act", lambda h: h.copy(out, in_), r=r, w=w)
        else:
            P.op(eng, lambda h: h.tensor_copy(out, in_), r=r, w=w)

    bank_rr = {"big": [0, [0, 1, 2]], "tr": [0, [3, 2]], "w": [0, [4, 5, 6, 7]]}

    def nbank(kind):
        st = bank_rr[kind]
        i = st[1][st[0] % len(st[1])]
        st[0] += 1
        return banks[i], "bank%d" % i

    dbg_outs = []

    def dbg(name, ap, keys, shape):
        if debug is None or name not in debug:
            return
        d = dout("dbg_" + name, shape)
        dma("sp", d, ap, r=keys, w=[], grp="dbg")
        dbg_outs.append("dbg_" + name)

    dma("sp", cst[:, :], cst_d[:, :], [], ["cst"], "c0")
    dma("sp", vecs[:, :], vecs_d[:, :], [], ["vecs"], "c0")
    dma("sp", gb[:, :], gb_d[:, :], [], ["gb"], "c0")
    dma("sp", tmS[:, :], wl_d[:, :], [], ["tmS"], "c0")
    cp("dve", identb[:, :], ident, ["cst"], ["identb"])
    cp("dve", onesb[:, :], onesblk, ["cst"], ["onesb"])
    cp("dve", Wlb[:, :], tmS[:, :], ["tmS"], ["Wlb"])
    ts("dve", vx[:, 0:4], vecs[:, V_KA:V_KA + 4], -1.0, None, ALU.mult, None, ["vecs"], ["vx"])
    ts("dve", vx[:, 4:21], vecs[:, V_MU:V_MU + 17], -1.0, 1.0, ALU.mult, ALU.add, ["vecs"], ["vx"])
    P.op("pool", lambda h: h.memset(carry[:, :], 0.0), r=[], w=["carry"])
    P.op("pool", lambda h: h.memset(Uext[:, :, 0:2], 0.0), r=[], w=["Uext0", "Uext1", "Uext2", "Uext3"])
    for hp in range(4):
        P.op("pool", lambda h, hp=hp: h.memset(SVb[hp][0][0:64, :], 0.0), r=[], w=["SV%d_0s" % hp])

    WCH = 128
    RORDER = [12, 0, 4, 8, 13, 1, 5, 9, 14, 2, 6, 10, 15, 3, 7, 11, 16]
    stg_bufs = [(O[:, :, :].rearrange("p a t -> p (a t)"), ["O0", "O1", "O2", "O3"]), (hbuf[:, :], ["hbuf"]),
                (xn32[:, :], ["xn32"])]
    g_order = [k * 4 + cg for cg in range(4) for k in range(4)] + [16 + r for r in RORDER]
    cast_eng = ["pool", "act", "dve"]
    stageq = []

    def mk_unit(wi, src_d, c0, dst, dkeys):
        def f():
            sbuf_, keys = stg_bufs[wi % 3]
            sv = sbuf_.rearrange("p (k c) -> p k c", k=8)
            dma("sp", sv, src_d[:, c0:c0 + WCH].rearrange("(k p) c -> p k c", p=128), [], keys, "stg%d" % (wi % 3))
            cp(cast_eng[wi % 3], dst[:, :, c0:c0 + WCH], sv, keys, dkeys)
        return f
    for wi, g in enumerate(g_order):
        stageq.append(mk_unit(wi, win_d, g * 128, Wb, ["Wb%d" % g]))
    for b_ in range(D // WCH):
        stageq.append(mk_unit(len(g_order) + b_, wout_d, b_ * WCH, Wo, ["Wo"]))
    STGK = []

    xt_i = [0]

    def norm_tile(tok, col, shift_rows=None, defer=None):
        xb = xt[xt_i[0] % 2]
        xk = "xt%d" % (xt_i[0] % 2)
        xt_i[0] += 1
        dma("sp", xb[:, :], x_d[tok:tok + 128, :], [], [xk], xk)
        act(xnb[:, :], xb[:, :], AF.Square, [xk], ["xnb", "st1"], accum_out=st1[:, 0:1])
        act(st1[:, 1:2], st1[:, 0:1], AF.Sqrt, ["st1"], ["st1"], bias=RMS_EPS, scale=1.0 / D)
        P.op("dve", lambda h: h.reciprocal(st1[:, 2:3], st1[:, 1:2]), r=["st1"], w=["st1"])
        stt(xnb[:, :], xb[:, :], st1[:, 2:3], gb[:, 0:D], ALU.mult, ALU.mult, [xk, "st1", "gb"], ["xnb"])
        if shift_rows is not None:
            stt(xn32[:, :], xb[:, :], st1[:, 2:3], gb[:, 0:D], ALU.mult, ALU.mult, [xk, "st1", "gb"], ["xn32"])
            shift_rows()
        if defer is not None:
            defer.append(lambda: norm_tile_b(col))
        else:
            norm_tile_b(col)

    def norm_tile_b(col):
        for half in range(2):
            bk, bkk = nbank("tr")
            for kk in range(4):
                k = half * 4 + kk
                mm(bk[:, kk * 128:(kk + 1) * 128], xnb[:, k * 128:(k + 1) * 128], identb[:, :],
                   ["xnb", "identb"], [bkk])
            cp("act" if half == 0 else "dve", xnT[:, half * 4:half * 4 + 4, col:col + 128],
               bk[:, :].rearrange("p (k t) -> p k t", k=4), [bkk], ["xnT"])

    def inproj(g, T):
        bk, bkk = nbank("big")
        for k in range(8):
            mm(bk[:, 0:T], Wb[:, k, g * 128:(g + 1) * 128], xnT[:, k, 0:T], ["Wb%d" % g, "xnT"], [bkk],
               start=(k == 0), stop=(k == 7))
        return bk, bkk

    def conv_branch(T, sample, yp=0, tick=lambda: None):
        Y = Yb[yp]
        for cg in range(4):
            uk = "Uext%d" % cg
            bk, bkk = inproj(cg, T)
            cp("act", hs[:, 0:T], bk[:, 0:T], [bkk], ["hs"])
            tick()
            bk, bkk = inproj(4 + cg, T)
            cp("act", bgs[:, 0:T], bk[:, 0:T], [bkk], ["bgs"])
            tick()
            bk, bkk = inproj(8 + cg, T)
            if not sample:
                tt("dve", Uext[:, cg, 2:2 + T], bk[:, 0:T], hs[:, 0:T], ALU.mult, [bkk, "hs"], [uk])
                u0, u1, u2 = Uext[:, cg, 0:T], Uext[:, cg, 1:T + 1], Uext[:, cg, 2:T + 2]
                zo = zt[:, 0:T]
            else:
                U3 = Uext[:, cg, 0:160].rearrange("p (b t) -> p b t", t=10)
                tt("dve", U3[:, :, 2:10], bk[:, 0:T].rearrange("p (b t) -> p b t", t=8),
                   hs[:, 0:T].rearrange("p (b t) -> p b t", t=8), ALU.mult, [bkk, "hs"], [uk])
                cp("pool", U3[:, :, 0:2], scT[:, cg, :].rearrange("p (b k) -> p b k", k=2), ["scT"], [uk])
                u0, u1, u2 = U3[:, :, 0:8], U3[:, :, 1:9], U3[:, :, 2:10]
                zo = zt[:, 0:T].rearrange("p (b t) -> p b t", t=8)
            tick()
            bk, bkk = inproj(12 + cg, T)
            act(sgc[:, 0:T], bk[:, 0:T], AF.Silu, [bkk], ["sgc"])
            tick()
            ts("dve", zo, u0, vcol(V_CW + 0 * 4 + cg), None, ALU.mult, None, [uk, "vecs"], ["zt"])
            stt(zo, u1, vcol(V_CW + 1 * 4 + cg), zo, ALU.mult, ALU.add, [uk, "vecs", "zt"], ["zt"])
            stt(zo, u2, vcol(V_CW + 2 * 4 + cg), zo, ALU.mult, ALU.add, [uk, "vecs", "zt"], ["zt"])
            tt("pool", zt[:, 0:T], zt[:, 0:T], bgs[:, 0:T], ALU.mult, ["zt", "bgs"], ["zt"])
            tt("pool", Y[:, cg, 0:T], zt[:, 0:T], sgc[:, 0:T], ALU.mult, ["zt", "sgc"], ["Y%d_%d" % (yp, cg)])

    def conv_carry(T):
        for cg in range(4):
            uk = "Uext%d" % cg
            cp("pool", Uext[:, cg, 0:2], Uext[:, cg, T:T + 2], [uk], [uk])

    def rwkv_proj(T, sample, tick=lambda n: None):
        for n, ri in enumerate(RORDER):
            tick(n)
            g = 16 + ri
            bk, bkk = inproj(g, T)
            pr = Praw[n % 2]
            pk = "Praw%d" % (n % 2)
            mu = vcol(V_MU + ri)
            omu = vx[:, 4 + ri:5 + ri]
            if not sample:
                ts("dve", pr[:, 1:T + 1], bk[:, 0:T], mu, None, ALU.mult, None, [bkk, "vecs"], [pk])
                cp("dve", pr[:, 0:1], carry[:, ri:ri + 1], ["carry"], [pk])
                stt(PM(ri, T), bk[:, 0:T], omu, pr[:, 0:T], ALU.mult, ALU.add, [bkk, pk, "vx"], ["PM%d" % ri] + STGK)
                cp("dve", carry[:, ri:ri + 1], pr[:, T:T + 1], [pk], ["carry"])
            else:
                act(pr[:, 1:T + 1], bk[:, 0:T], AF.Copy, [bkk, "vecs"], [pk], scale=mu)
                d3 = dtmp[:, 0:T].rearrange("p (b t) -> p b t", t=8)
                p3 = pr[:, 1:T + 1].rearrange("p (b t) -> p b t", t=8)
                cp("pool", d3[:, :, 1:8], p3[:, :, 0:7], [pk], ["dtmp"])
                act(d3[:, :, 0:1], Pfirst[:, ri, :].rearrange("p (b o) -> p b o", o=1), AF.Copy, ["Pfirst", "vecs"], ["dtmp"],
                    scale=mu)
                stt(PM(ri, T), bk[:, 0:T], omu, dtmp[:, 0:T], ALU.mult, ALU.add, [bkk, "dtmp", "vx"], ["PM%d" % ri] + STGK)
            if ri == 12:
                lora_prep(T)

    def lora_prep(T):
        act(LAb[0:64, 0:T], PMW[0:64, 12 * TB:12 * TB + T], AF.Tanh, ["PM12"], ["LA"])
        cp("dve", LAb[64:128, 0:T], PMW[64:128, 12 * TB:12 * TB + T], ["PM12"], ["LA"])

    def prep_steps(hp, T, sample, sb_, boff=0):
        rk = ["PM%d" % hp]
        kk_ = ["PM%d" % (4 + hp)]
        vk = ["PM%d" % (8 + hp)]
        gk = ["PM%d" % (13 + hp)]
        r_ = PM(hp, T); k_ = PM(4 + hp, T); v_ = PM(8 + hp, T); g_ = PM(13 + hp, T)
        eL = R_eL[sb_]; bon = R_bon[sb_][:, boff:]; sg = R_sg[sb_][:, boff:]
        eLk = "eL%d" % sb_; bonk = "bon%d" % sb_; sgk = "sg%d" % sb_; ARk = "AR%d" % sb_; KBk = "KB%d" % sb_
        AR = R_AR[sb_].rearrange("p (a t) -> p a t", a=2)
        KB = R_KB[sb_].rearrange("p (a t) -> p a t", a=2)
        def A():
            bk, bkk = nbank("tr")
            mm(bk[:, 0:T], Wlb[0:64, hp * 128:(hp + 1) * 128], LAb[0:64, 0:T], ["Wlb", "LA"], [bkk])
            act(R_sigw[:, 0:T], bk[:, 0:T], AF.Sigmoid, [bkk, "vecs"], ["sigw"], bias=vcol(V_W0 + hp))
            bk, bkk = nbank("tr")
            mm(bk[:, 0:T], Wlb[64:128, hp * 128:(hp + 1) * 128], LAb[64:128, 0:T], ["Wlb", "LA"], [bkk])
            act(R_a[:, 0:T], bk[:, 0:T], AF.Sigmoid, [bkk, "vecs"], ["a"], bias=vcol(V_A0 + hp))
            ts("dve", R_t1[:, 0:T], k_, vcol(V_KK + hp), None, ALU.mult, None, kk_ + ["vecs"], ["t1"])
            tt("pool", sqb[:, 0, 0:T], R_t1[:, 0:T], R_t1[:, 0:T], ALU.mult, ["t1"], ["sq0"])

        def B1():
            ts("dve", R_t5[:, 0:T], R_a[:, 0:T], vcol(V_KA + hp), vx[:, hp:hp + 1], ALU.mult, ALU.add, ["a", "vecs", "vx"], ["t5"])
            stt(R_kmod[:, 0:T], R_t5[:, 0:T], 1.0, k_, ALU.add, ALU.mult, ["t5"] + kk_, ["kmod"])
            act(R_t6[:, 0:T], R_sigw[:, 0:T], AF.Copy, ["sigw"], ["t6"], scale=-DECAY_SCALE)

        def B2():
            bk, bkk = nbank("tr")
            mm(bk[:, 0:T], onesb[:, :], sqb[:, 0, 0:T], ["onesb", "sq0"], [bkk])
            act(R_t2[:, 0:T], bk[:, 0:T], AF.Sqrt, [bkk], ["t2"])

        def C1():
            stt(sqb[:, 1, 0:T], r_, vcol(V_RK + hp), R_kmod[:, 0:T], ALU.mult, ALU.mult, rk + ["kmod", "vecs"], ["sq1"])

        def C2():
            ts("dve", R_t2[:, 0:T], R_t2[:, 0:T], 1e-12, None, ALU.max, None, ["t2"], ["t2"])
            P.op("dve", lambda h: h.reciprocal(R_t2[:, 0:T], R_t2[:, 0:T]), r=["t2"], w=["t2"])
            tt("pool", R_kkn[:, 0:T], R_t1[:, 0:T], R_t2[:, 0:T], ALU.mult, ["t1", "t2"], ["kkn"])
            tt("pool", R_t3[:, 0:T], R_kkn[:, 0:T], R_a[:, 0:T], ALU.mult, ["kkn", "a"], ["t3"])

        def Dd():
            P.op("dve", lambda h: h.tensor_tensor_scan(R_L[:, 0:T], SCANM[:, 0:T], R_t6[:, 0:T], 0.0, ALU.mult, ALU.add),
                 r=["cst", "t6"], w=["L"])
            tt("pool", R_t5[:, 0:T], R_L[:, 0:T], R_t6[:, 0:T], ALU.subtract, ["L", "t6"], ["t5"])
            act(eL[:, 0:T], R_L[:, 0:T], AF.Exp, ["L"], [eLk])
            act(R_enL[:, 0:T], R_L[:, 0:T], AF.Exp, ["L"], ["enL"], scale=-1.0)
            act(R_t5[:, 0:T], R_t5[:, 0:T], AF.Exp, ["t5"], ["t5"])

        def E():
            stt(AR[:, 0, 0:T], R_kkn[:, 0:T], -1.0, R_t5[:, 0:T], ALU.mult, ALU.mult, ["kkn", "t5"], [ARk])
            tt("pool", AR[:, 1, 0:T], r_, eL[:, 0:T], ALU.mult, rk + [eLk], [ARk])
            tt("pool", KB[:, 0, 0:T], R_kmod[:, 0:T], R_enL[:, 0:T], ALU.mult, ["kmod", "enL"], [KBk])
            tt("dve", KB[:, 1, 0:T], R_t3[:, 0:T], R_enL[:, 0:T], ALU.mult, ["t3", "enL"], [KBk])

        def F():
            bk, bkk = nbank("tr")
            mm(bk[:, 0:T], onesb[:, :], sqb[:, 1, 0:T], ["onesb", "sq1"], [bkk])
            tt("dve", bon[:, 0:T], bk[:, 0:T], v_, ALU.mult, [bkk] + vk, [bonk])
            act(sg[:, 0:T], g_, AF.Silu, gk, [sgk])
        if sample:
            return [A, B1, B2, C1, C2, F]
        return [A, B1, B2, C1, C2, Dd, E, F]

    def wkv_pre(hp, blk, T, sb_, tick):
        AR = R_AR[sb_].rearrange("p (a t) -> p a t", a=2)
        KB = R_KB[sb_].rearrange("p (a t) -> p a t", a=2)
        ARk = "AR%d" % sb_; KBk = "KB%d" % sb_; eLk = "eL%d" % sb_
        eL = R_eL[sb_]
        v_ = PM(8 + hp, T)
        vkey = "PM%d" % (8 + hp)
        nch = T // C
        inst = list(range(nch))
        cs = [slice(c * C, (c + 1) * C) for c in range(nch)]
        AAk = ["AA%d" % i for i in inst]
        HS = [slice(0, 64), slice(64, 128)]
        for i in inst:
            bk, bkk = nbank("w")
            for ps in HS:
                mm(bk[ps, 0:128].rearrange("p (a t) -> p a t", a=2), KB[ps, 1, cs[i]], AR[ps, :, cs[i]], [KBk, ARk], [bkk])
                mm(bk[ps, 128:256].rearrange("p (a t) -> p a t", a=2), KB[ps, 0, cs[i]], AR[ps, :, cs[i]], [KBk, ARk], [bkk])
            tt("dve", I_AA[i], bk[:, 0:256], MASK_A, ALU.mult, [bkk, "cst"], [AAk[i]])
            if i % 2 == 1:
                tick()
        zp = [0] * nch
        for i in inst:
            bk, bkk = nbank("w")
            for ps in HS:
                mm(bk[ps, 0:64], AR[ps, 0, cs[i]], identb[ps, ps], [ARk, "identb"], [bkk])
                mm(bk[ps, 64:192].rearrange("p (a t) -> p a t", a=2), AR[ps, 0, cs[i]], KB[ps, :, cs[i]], [ARk, KBk], [bkk])
            tt("dve", I_Z[i][0][:, 0:192], bk[:, 0:192], MASK_Z, ALU.mult, [bkk, "cst"], ["Z%d_0" % i])
            if i % 2 == 1:
                tick()
        for i in inst:
            wc = eL[:, i * C + C - 1:i * C + C]
            BKt = I_BKt[i].rearrange("p (a t) -> p a t", a=2)
            ts("dve", BKt[:, :, :], KB[:, :, cs[i]], wc, None, ALU.mult, None, [KBk, eLk], ["BKt%d" % i])
            act(I_DW[i], IDENT2, AF.Copy, ["cst", eLk], ["DW%d" % i], scale=wc)
            bk, bkk = nbank("w")
            for ps in HS:
                mm(bk[ps, 0:64], BKt[ps, 1, :], identb[ps, ps], ["BKt%d" % i, "identb"], [bkk])
            mm(bk[64:128, 64:192], v_[:, cs[i]], ident, [vkey, "cst"], [bkk])
            cp("act", I_BT[i], bk[:, 0:64], [bkk], ["BT%d" % i])
            cp("act", SVb[hp][i][64:128, :], bk[64:128, 64:192], [bkk], ["SV%d_%dv" % (hp, i)])
            if i % 2 == 1:
                tick()
        for lv in range(6):
            for i in inst:
                zi = I_Z[i][zp[i]]
                zo = I_Z[i][1 - zp[i]]
                zik = "Z%d_%d" % (i, zp[i])
                zok = "Z%d_%d" % (i, 1 - zp[i])
                bk, bkk = nbank("w")
                for ps in HS:
                    if lv == 0:
                        nat, natk = I_AA[i][ps, 0:64], AAk[i]
                    else:
                        nat, natk = zi[ps, 192:256], zik
                    mm(bk[ps, 0:128], identb[ps, ps], zi[ps, 0:128], ["identb", zik], [bkk], start=True, stop=False)
                    mm(bk[ps, 0:128], nat, zi[ps, 0:128], [natk, zik], [bkk], start=False, stop=(lv == 5))
                    if lv < 5:
                        mm(bk[ps, 128:192], nat, zi[ps, 128:192], [natk, zik], [bkk], start=False, stop=False)
                        mm(bk[ps, 192:256], zi[ps, 128:192], nat, [natk, zik], [bkk], start=False, stop=True)
                ncol = 256 if lv < 5 else 128
                cp("dve" if (i + lv) % 2 == 0 else "act", zo[:, 0:ncol], bk[:, 0:ncol], [bkk], [zok])
                zp[i] = 1 - zp[i]
            tick()
        for i in inst:
            Yk = "Z%d_%d" % (i, zp[i])
            Yt = I_Z[i][zp[i]]
            BKt = I_BKt[i].rearrange("p (a t) -> p a t", a=2)
            for hh in range(2):
                ps = HS[hh]
                bk, bkk = nbank("w")
                mm(bk[:, 0:64], Yt[ps, 0:128], I_BT[i][ps, :], [Yk, "BT%d" % i], [bkk], start=True, stop=False)
                mm(bk[:, 64:128], Yt[ps, 0:128], I_AA[i][ps, 64:128], [Yk, AAk[i]], [bkk], start=False, stop=False)
                mm(bk[0:64, 0:64], identb[ps, ps], I_DW[i][ps, :], ["identb", "DW%d" % i], [bkk], start=False, stop=False)
                mm(bk[0:64, 64:128], identb[ps, ps], AR[ps, 1, cs[i]], ["identb", ARk], [bkk], start=False, stop=True)
                mm(bk[64:128, 0:64], BKt[ps, 0, :], identb[ps, ps], ["BKt%d" % i, "identb"], [bkk], start=False, stop=False)
                mm(bk[64:128, 64:128], identb[ps, ps], I_AA[i][ps, 192:256], ["identb", AAk[i]], [bkk], start=False,
                   stop=True)
                cp("act" if hh == 0 else "dve", I_EF[i][hh], bk[:, 0:128], [bkk], ["EF%d_%d" % (i, hh)])
            tick()

    def chain_steps(hp, T, last_block):
        nch = T // C
        HS = [slice(0, 64), slice(64, 128)]
        cs = [slice(c * C, (c + 1) * C) for c in range(nch)]
        S = []

        def step(i):
            sv = SVb[hp][i].rearrange("p (h c) -> p h c", h=2)
            svn = SVb[hp][(i + 1) % 4]
            svk = ["SV%d_%ds" % (hp, i), "SV%d_%dv" % (hp, i)]
            bk, bkk = nbank("w")
            for hh in range(2):
                mm(bk[0:64, hh * 64:hh * 64 + 64], I_EF[i][hh][:, 0:64], sv[:, hh, :], ["EF%d_%d" % (i, hh)] + svk, [bkk])
            for hh in range(2):
                ps = HS[hh]
                mm(bk[ps, 128:192], sv[:, hh, :], I_EF[i][hh][:, 64:128], ["EF%d_%d" % (i, hh)] + svk, [bkk])
            cp("dve", svn[0:64, :], bk[0:64, 0:128], [bkk], ["SV%d_%ds" % (hp, (i + 1) % 4)])
            cp("act", O[:, hp, cs[i]], bk[:, 128:192], [bkk], ["O%d" % hp])
        for i in range(nch):
            S.append(lambda i=i: step(i))

        def fin():
            par = 0
            sv = SVb[hp][par].rearrange("p (h c) -> p h c", h=2)
            bk, bkk = nbank("w")
            for hh in range(2):
                mm(bk[0:64, hh * 64:hh * 64 + 64], sv[0:64, hh, :], ident[0:64, 0:64], ["SV%d_%ds" % (hp, par), "cst"],
                   [bkk])
            cp("dve", tmS[0:64, hp * 128:hp * 128 + 128], bk[0:64, 0:128], [bkk], ["tmS"])
            for hh in range(2):
                h_ = hp * 2 + hh
                dma("sp", wkvp_d[h_ * 64:(h_ + 1) * 64, :], tmS[0:64, hp * 128 + hh * 64:hp * 128 + hh * 64 + 64],
                    ["tmS"], [], "o_wkvp")
        if last_block:
            S.append(fin)
        return S

    def post_steps(hp, T, sb_, yp=0, boff=0):
        Y = Yb[yp]
        Oh = O[:, hp, 0:T]
        ok = "O%d" % hp
        bon = R_bon[sb_][:, boff:]; sg = R_sg[sb_][:, boff:]
        bonk = "bon%d" % sb_; sgk = "sg%d" % sb_

        def p0():
            bk, bkk = nbank("tr")
            mm(bk[:, 0:T], onesblk, Oh, ["cst", ok], [bkk])
            stt(P_t1[:, 0:T], bk[:, 0:T], -1.0 / 64, Oh, ALU.mult, ALU.add, [bkk, ok], ["pt1"])
            tt("pool", P_t2.bitcast(BF16)[:, 0:T], P_t1[:, 0:T], P_t1[:, 0:T], ALU.mult, ["pt1"], ["pt2"])

        def p1():
            bk, bkk = nbank("tr")
            mm(bk[:, 0:T], onesb[:, :], P_t2.bitcast(BF16)[:, 0:T], ["onesb", "pt2"], [bkk])
            act(P_t2[:, 0:T], bk[:, 0:T], AF.Sqrt, [bkk], ["pt2"], bias=GN_EPS, scale=1.0 / 64)
            P.op("dve", lambda h: h.reciprocal(P_t2[:, 0:T], P_t2[:, 0:T]), r=["pt2"], w=["pt2"])
            tt("dve", P_t1[:, 0:T], P_t1[:, 0:T], P_t2[:, 0:T], ALU.mult, ["pt1", "pt2"], ["pt1"])
            ts("dve", P_t1[:, 0:T], P_t1[:, 0:T], vcol(V_LG + hp), vcol(V_LB + hp), ALU.mult, ALU.add, ["pt1", "vecs"], ["pt1"])
            tt("pool", P_t1[:, 0:T], P_t1[:, 0:T], bon[:, 0:T], ALU.add, ["pt1", bonk], ["pt1"])
            tt("dve", Y[:, 4 + hp, 0:T], P_t1[:, 0:T], sg[:, 0:T], ALU.mult, ["pt1", sgk], ["Y%d_%d" % (yp, 4 + hp)])
        return [p0, p1]

    def post(hp, T, sb_, yp=0, boff=0):
        for st in post_steps(hp, T, sb_, yp, boff):
            st()

    def outproj(tok, col, yp=0):
        Y = Yb[yp]
        dma("sp", hbuf[:, :], x_d[tok:tok + 128, :], [], ["hbuf"], "hbuf")
        for half in range(2):
            bk, bkk = nbank("big")
            for g in range(8):
                mm(bk[:, :], Y[:, g, col:col + 128], Wo[:, g, half * 512:(half + 1) * 512], ["Y%d_%d" % (yp, g), "Wo"], [bkk],
                   start=(g == 0), stop=(g == 7))
            tt("dve", hbuf[:, half * 512:(half + 1) * 512], hbuf[:, half * 512:(half + 1) * 512], bk[:, :], ALU.add,
               ["hbuf", bkk], ["hbuf"])
        act(xnb[:, :], hbuf[:, :], AF.Square, ["hbuf"], ["xnb", "st2"], accum_out=st2[:, 0:1])
        act(st2[:, 1:2], st2[:, 0:1], AF.Sqrt, ["st2"], ["st2"], bias=RMS_EPS, scale=1.0 / D)
        P.op("dve", lambda h: h.reciprocal(st2[:, 2:3], st2[:, 1:2]), r=["st2"], w=["st2"])
        stt(hbuf[:, :], hbuf[:, :], st2[:, 2:3], gb[:, D:2 * D], ALU.mult, ALU.mult, ["hbuf", "st2", "gb"], ["hbuf"])
        dma("sp", y_d[tok:tok + 128, :], hbuf[:, :], ["hbuf"], [], "hbuf")

    def transpose_to_tm(src, srck, dstcols, T=128):
        bk, bkk = nbank("tr")
        mm(bk[:, 0:128], src, ident, srck + ["cst"], [bkk])
        cp("act", tmS[:, dstcols], bk[:, 0:128], [bkk], ["tmS"])

    def cut(n):
        if stop_at is not None and n == stop_at:
            if not P.dead:
                print("CUT", n, "nops", P.nops)
            P.dead = True

    cut(0)
    def norm_items(blk):
        tok0 = blk * TB
        items = []
        for t in range(2):
            def A(t=t):
                dl = []
                if blk == NBP - 1 and t == 1:
                    def srow():
                        dma("sp", shiftp_d[0:1, :], xn32[127:128, :], ["xn32"], [], "o_misc")
                    norm_tile(tok0 + t * 128, t * 128, srow, defer=dl)
                else:
                    norm_tile(tok0 + t * 128, t * 128, defer=dl)
                pendB.append(dl[0])

            def B():
                pendB.pop(0)()
            items += [A, B]
        return items
    pendB = []

    def norm_block(blk):
        for it in norm_items(blk):
            it()

    norm_block(0)
    for _ in range(3):
        stageq.pop(0)()
    tailq = []
    for blk in range(NBP):
        T = TB
        tok0 = blk * TB
        last = blk == NBP - 1
        yp = blk % 2

        tcnt = [0]

        def ttick(tailq=tailq, tcnt=tcnt):
            if stageq:
                stageq.pop(0)()
            tcnt[0] += 1
            if tailq and tcnt[0] % 2 == 1:
                tailq.pop(0)()
        cut(1 if blk == 0 else -1)
        conv_branch(T, False, yp, ttick)
        cut(2 if blk == 0 else -1)
        if last:
            for cg in range(4):
                transpose_to_tm(Uext[:, cg, 2 + 128:2 + 256], ["Uext%d" % cg], slice(cg * 128, cg * 128 + 128))
            dma("sp", convp_d[:, :], tmS[126:128, :], ["tmS"], [], "o_misc")
        conv_carry(T)
        p0 = prep_steps(0, T, False, 0)

        def rtick(n, tailq=tailq, p0=p0):
            if stageq:
                stageq.pop(0)()
            if tailq:
                tailq.pop(0)()
            elif n >= 6 and p0:
                p0.pop(0)()
        rwkv_proj(T, False, rtick)
        while stageq:
            stageq.pop(0)()
        while tailq:
            tailq.pop(0)()
        cut(3 if blk == 0 else -1)
        while p0:
            p0.pop(0)()
        pend_chain, pend_post = [], []
        for hp in range(4):
            if hp < 3:
                other = prep_steps(hp + 1, T, False, (hp + 1) % 2)
            elif not last:
                other = norm_items(blk + 1)
            else:
                other = []
            q = []
            oth = list(other)
            last_item = [oth.pop()] if oth else []
            for c in pend_chain:
                if oth:
                    q.append(oth.pop(0))
                q.append(c)
            posts = list(pend_post)
            while oth or posts:
                if oth:
                    q.append(oth.pop(0))
                if posts and (len(oth) <= 1 or len(q) >= 11):
                    q.append(posts.pop(0))
                    if oth:
                        q.append(oth.pop(0))
            q += last_item

            def tick(q=q):
                if q:
                    q.pop(0)()
            wkv_pre(hp, blk, T, hp % 2, tick)
            while q:
                q.pop(0)()
            pend_chain = chain_steps(hp, T, last)
            pend_post = post_steps(hp, T, hp % 2, yp)
        tailq.extend(pend_chain + pend_post)
        for t in range(2):
            tailq.append(lambda t=t, tok0=tok0, yp=yp: outproj(tok0 + t * 128, t * 128, yp))
        cut(7 if blk == 0 else -1)
    cut(8)

    def global_barrier():
        for e in ("pe", "act", "dve", "pool", "sp"):
            for f in ("pe", "act", "dve", "pool"):
                if f != e and P.cnt[f] > 0:
                    P._wait(e, ("e", f, P.cnt[f]))
            P.barrier_all_dma(e)
            P.flush(e)

    def tkS(n=None):
        if tailq:
            tailq.pop(0)()
    T = TS
    tok0 = TP
    dma("sp", xn32[0:16, :], sshift_d[:, :], [], ["xn32"], "c1")
    cp("pool", xnb[0:16, :], xn32[0:16, :], ["xn32"], ["xnb"])
    for half in range(2):
        bk, bkk = nbank("tr")
        for kk in range(4):
            k = half * 4 + kk
            mm(bk[:, kk * 16:(kk + 1) * 16], xnb[0:16, k * 128:(k + 1) * 128], identb[0:16, 0:16], ["xnb", "identb"], [bkk])
        cp("act", ssT[:, half * 4:half * 4 + 4, :], bk[:, 0:64].rearrange("p (k t) -> p k t", k=4), [bkk], ["ssT"])
    for ri in range(17):
        g = 16 + ri
        bk, bkk = nbank("big")
        for k in range(8):
            mm(bk[:, 0:16], Wb[:, k, g * 128:(g + 1) * 128], ssT[:, k, :], ["Wb%d" % g, "ssT"], [bkk], start=(k == 0),
               stop=(k == 7))
        cp("act", Pfirst[:, ri, :], bk[:, 0:16], [bkk], ["Pfirst"])
    dma("sp", xn32[0:32, 0:512], sconv_d[:, :], ["xnb"], ["xn32"], "c1")
    for cg in range(4):
        bk, bkk = nbank("tr")
        mm(bk[:, 0:32], xn32[0:32, cg * 128:(cg + 1) * 128], ident[0:32, 0:32], ["xn32", "cst"], [bkk])
        cp("act", scT[:, cg, :], bk[:, 0:32], [bkk], ["scT"])

    def srow_s():
        for b in range(16):
            dma("sp", shifts_d[b:b + 1, :], xn32[b * 8 + 7:b * 8 + 8, :], ["xn32"], [], "o_misc")
    norm_tile(tok0, 0, srow_s)
    conv_branch(T, True, 0, tkS)
    for cg in range(4):
        cp("pool", dtmp[:, 0:128].rearrange("p (b t) -> p b t", t=8),
           Uext[:, cg, 0:160].rearrange("p (b t) -> p b t", t=10)[:, :, 2:10], ["Uext%d" % cg], ["dtmp"])
        transpose_to_tm(dtmp[:, 0:128], ["dtmp"], slice(cg * 128, cg * 128 + 128))
    for b in range(16):
        dma("sp", convs_d[2 * b:2 * b + 2, :], tmS[b * 8 + 6:b * 8 + 8, :], ["tmS"], [], "o_misc")
    rwkv_proj(T, True, tkS)
    while tailq:
        tailq.pop(0)()
    PMJ = S_pmj.rearrange("p (t q j) -> p t q j", t=8, q=6)
    for hp in range(4):
        for st in prep_steps(hp, T, True, hp // 2, (hp % 2) * 128):
            st()
        act(R_eL[0][:, 0:T], R_sigw[:, 0:T], AF.Exp, ["sigw"], ["eL0"], scale=-DECAY_SCALE)
        srcs = [(PM(hp, T), ["PM%d" % hp]), (R_eL[0][:, 0:T], ["eL0"]), (R_kmod[:, 0:T], ["kmod"]),
                (PM(8 + hp, T), ["PM%d" % (8 + hp)]), (R_kkn[:, 0:T], ["kkn"]), (R_t3[:, 0:T], ["t3"])]
        for qi, (src, sk) in enumerate(srcs):
            sl = (hp * 6 + qi) % 4
            tsl = tmS[:, sl * 128:(sl + 1) * 128]
            bk, bkk = nbank("tr")
            mm(bk[:, 0:128], src, ident, sk + ["cst"], [bkk])
            cp("act" if qi % 2 == 0 else "dve", tsl, bk[:, 0:128], [bkk], ["tmS%d" % sl, "tmS"])
            dst = scr1_d[:, hp * 768:(hp + 1) * 768].rearrange("r (h c) -> r h c", h=2)[:, :, qi * 64:(qi + 1) * 64]
            dma("sp" if qi % 2 == 0 else "act", dst, tsl.rearrange("r (h j) -> r h j", h=2), ["tmS%d" % sl], [], "scr1w")
    global_barrier()
    for b in range(16):
        src = scr1_d[b * 8:(b + 1) * 8, :].rearrange("t (h c) -> h t c", h=8)
        dma("sp" if b % 2 == 0 else "act", S_pmj[b * 8:(b + 1) * 8, :].rearrange("p (t c) -> p t c", t=8), src, [], ["pmj"], "pmj")
    Opm = S_opm.rearrange("p (t i) -> p t i", t=8)
    for ib in range(4):
        Sb = S_blk[0]
        sk = "Sblk0"
        tmpb = S_tmp[0]
        tmpc = S_tmp[1]
        dma("sp", Sb, swkv_d[:, ib * 1024:(ib + 1) * 1024], [], [sk], sk)
        S3 = Sb.rearrange("p (i j) -> p i j", j=64)
        T3 = tmpb.rearrange("p (i j) -> p i j", j=64)
        U3 = tmpc.rearrange("p (i j) -> p i j", j=64)

        def bj(q, t):
            return PMJ[:, t, q, :].unsqueeze(1).to_broadcast([128, 16, 64])

        def bi(ap2):
            return ap2.unsqueeze(2).to_broadcast([128, 16, 64])
        for t in range(8):
            tt("dve", T3, S3, bj(4, t), ALU.mult, [sk, "pmj"], ["tmpb"])
            P.op("dve", lambda h, T3=T3: h.tensor_reduce(S_sa[:, 0:16], T3, AX.X, ALU.add, negate=True), r=["tmpb"], w=["sa"])
            tt("pool", U3, bi(PMJ[:, t, 3, ib * 16:(ib + 1) * 16]), bj(2, t), ALU.mult, ["pmj"], ["tmpc"])
            tt("dve", S3, S3, bj(1, t), ALU.mult, [sk, "pmj"], [sk])
            tt("dve", T3, bi(S_sa[:, 0:16]), bj(5, t), ALU.mult, ["sa", "pmj"], ["tmpb"])
            tt("dve", S3, S3, T3, ALU.add, [sk, "tmpb"], [sk])
            tt("dve", S3, S3, U3, ALU.add, [sk, "tmpc"], [sk])
            tt("dve", T3, S3, bj(0, t), ALU.mult, [sk, "pmj"], ["tmpb"])
            P.op("dve", lambda h, T3=T3, t=t, ib=ib: h.tensor_reduce(Opm[:, t, ib * 16:(ib + 1) * 16], T3, AX.X, ALU.add),
                 r=["tmpb"], w=["opm"])
        dma("sp", wkvs_d[:, ib * 1024:(ib + 1) * 1024], Sb, [sk], [], sk)
    for b in range(16):
        dst = scr2_d[b * 8:(b + 1) * 8, :].rearrange("t (h i) -> h t i", h=8)
        dma("sp" if b % 2 == 0 else "act", dst, Opm[b * 8:(b + 1) * 8, :, :], ["opm"], [], "scr2w")
    global_barrier()
    dma("sp", tmS[:, :], scr2_d[:, :], [], ["tmS"], "c1")
    for hp in range(4):
        bk, bkk = nbank("tr")
        mm(bk[:, 0:128], tmS[:, hp * 128:(hp + 1) * 128], ident, ["tmS", "cst"], [bkk])
        cp("act", O[:, hp, 0:128], bk[:, 0:128], [bkk], ["O%d" % hp])
    for hp in range(4):
        post(hp, T, hp // 2, 0, (hp % 2) * 128)
    outproj(tok0, 0)

    P.dead = False
    for e in ("sp",):
        for f in ("pe", "act", "dve", "pool"):
            P._wait(e, ("e", f, P.cnt[f]))
        P.barrier_all_dma(e)
        P.flush(e)

    with nc.Block() as block:
        @block.sync
        def _(e):
            for t in P.q["sp"]:
                t(e)

        @block.tensor
        def _(e):
            for t in P.q["pe"]:
                t(e)

        @block.scalar
        def _(e):
            for t in P.q["act"]:
                t(e)

        @block.vector
        def _(e):
            for t in P.q["dve"]:
                t(e)

        @block.gpsimd
        def _(e):
            for t in P.q["pool"]:
                t(e)
    es.close()
    return nc, dbg_outs


def _consts():
    c = np.zeros((128, 1024), np.float32)
    p = np.arange(128)
    c[:, 0:128] = np.eye(128, dtype=np.float32)
    c[:, 128:256] = (p[:, None] // 64 == p[None, :] // 64).astype(np.float32)
    s = (p % 64)[:, None]
    t = np.arange(64)[None, :]
    strict = (s < t).astype(np.float32)
    incl = (s <= t).astype(np.float32)
    c[:, 256:512] = np.concatenate([strict, incl, strict, incl], axis=1)
    lower = (s > t).astype(np.float32)
    c[:, 512:704] = np.concatenate([np.ones((128, 64), np.float32), lower, lower], axis=1)
    c[:, 704:960] = np.tile((np.arange(256) % 64 != 0).astype(np.float32)[None, :], (128, 1))
    c[:, 960:1024] = (s == t).astype(np.float32)
    return c


def _prep_inputs(inp):
    f = lambda a: np.ascontiguousarray(np.asarray(a, dtype=np.float32))
    xp = f(inp["x_prompt"]); xs = f(inp["x_sample"])
    sc = f(inp["state_conv"])[0]; ss = f(inp["state_shift"])[0]; sw = f(inp["state_wkv"])[0]
    colv = lambda v, n: f(v).reshape(n, 128).T
    vecs = np.zeros((128, NV), np.float32)
    vecs[:, V_MU:V_MU + 17] = colv(inp["mu_shift"][0], 17)
    cw = f(inp["conv_w"])[0]
    for k in range(3):
        vecs[:, V_CW + k * 4:V_CW + k * 4 + 4] = colv(cw[k], 4)
    vecs[:, V_W0:V_W0 + 4] = colv(inp["w0"][0], 4)
    vecs[:, V_A0:V_A0 + 4] = colv(inp["a0"][0], 4)
    vecs[:, V_KK:V_KK + 4] = colv(inp["k_k"][0], 4)
    vecs[:, V_KA:V_KA + 4] = colv(inp["k_a"][0], 4)
    vecs[:, V_RK:V_RK + 4] = colv(f(inp["r_k"])[0].reshape(512), 4)
    vecs[:, V_LG:V_LG + 4] = colv(inp["lnx_g"][0], 4)
    vecs[:, V_LB:V_LB + 4] = colv(inp["lnx_b"][0], 4)
    gbv = np.concatenate([f(inp["norm_g"])[0], f(inp["final_norm_g"])])[None, :]
    gbv = np.ascontiguousarray(np.broadcast_to(gbv, (128, 2 * D)))
    wl = np.ascontiguousarray(np.concatenate([f(inp["w_dec2"])[0], f(inp["w_a2"])[0]], axis=0))
    cst = _consts()
    win = f(inp["w_in"])[0]; wout = f(inp["w_out"])[0]
    maps = []
    for c in range(NCORES):
        xa = np.ascontiguousarray(np.concatenate([xp[c], xs[16 * c:16 * c + 16].reshape(128, D)], axis=0))
        maps.append({
            "x": xa,
            "sconv": np.ascontiguousarray(sc[16 * c:16 * c + 16].reshape(32, 512)),
            "sshift": np.ascontiguousarray(ss[16 * c:16 * c + 16]),
            "swkv": np.ascontiguousarray(sw[16 * c:16 * c + 16].reshape(128, 4096)),
            "w_in": win, "w_out": wout, "wl": wl, "vecs": vecs, "gb": gbv, "cst": cst,
        })
    return maps


_NC_CACHE = {}


def kernel(**inputs):
    if "nc" not in _NC_CACHE:
        _NC_CACHE["nc"] = build()[0]
    nc = _NC_CACHE["nc"]
    maps = _prep_inputs(inputs)
    res = run_bass_kernel_spmd(nc, maps, core_ids=list(range(NCORES)))
    R = res.results
    y_p = np.stack([R[c]["y"][0:TP] for c in range(NCORES)], axis=0)
    y_s = np.concatenate([R[c]["y"][TP:TT].reshape(16, 8, D) for c in range(NCORES)], axis=0)
    conv_p = np.stack([R[c]["conv_p"] for c in range(NCORES)], axis=0)[None]
    shift_p = np.stack([R[c]["shift_p"][0] for c in range(NCORES)], axis=0)[None]
    wkv_p = np.stack([R[c]["wkv_p"].reshape(8, 64, 64) for c in range(NCORES)], axis=0)[None]
    conv_s = np.concatenate([R[c]["conv_s"].reshape(16, 2, 512) for c in range(NCORES)], axis=0)[None]
    shift_s = np.concatenate([R[c]["shift_s"] for c in range(NCORES)], axis=0)[None]
    wkv_s = np.concatenate([R[c]["wkv_s"].reshape(16, 8, 64, 64) for c in range(NCORES)], axis=0)[None]
    return tuple(np.ascontiguousarray(a.astype(np.float32)) for a in
                 (y_p, y_s, conv_p, shift_p, wkv_p, conv_s, shift_s, wkv_s))
```
